# Optimizing a Trainium2 kernel written in Bass

```python
import math
import jax, jax.numpy as jnp
from jax import lax
import numpy as np

D_MODEL = 1024
BATCH = 4
SEQ = 4096
DEPTH = 1

CHUNK = 64
Q_BLOCK = 128
N_MEM = 256
MIX_WIDTH = D_MODEL
FOX_WIDTH = MIX_WIDTH // 2
HGRN_WIDTH = MIX_WIDTH - FOX_WIDTH
FOX_HEAD_DIM = 64
FOX_HEADS = FOX_WIDTH // FOX_HEAD_DIM
HGRN_KEY_DIM = 128
HGRN_HEADS = HGRN_WIDTH // HGRN_KEY_DIM
HGRN_VAL_DIM = HGRN_WIDTH // HGRN_HEADS
X_HEADS = 4
X_HEAD_DIM = D_MODEL // X_HEADS
D_FF = 4 * D_MODEL
EPS = 1e-6
SPLITS = (FOX_WIDTH, FOX_WIDTH, FOX_WIDTH, FOX_HEADS, HGRN_WIDTH, HGRN_WIDTH, HGRN_WIDTH, HGRN_WIDTH)
IN_COLS = sum(SPLITS)

kernel_name = "hymba_fox_hgrn2_memory_block"


def rmsnorm(x, g):
    xf = x.astype(jnp.float32)
    y = xf * lax.rsqrt(jnp.mean(xf * xf, axis=-1, keepdims=True) + EPS)
    return (y * g.astype(jnp.float32)).astype(x.dtype)


def fox_attention(q, k, v, log_f):
    b_, h_, s_, d_ = q.shape
    nb = s_ // Q_BLOCK
    scale = 1.0 / math.sqrt(d_)
    F = jnp.cumsum(log_f, axis=-1)
    qb = q.reshape(b_, h_, nb, Q_BLOCK, d_).transpose(2, 0, 1, 3, 4)
    Fb = F.reshape(b_, h_, nb, Q_BLOCK).transpose(2, 0, 1, 3)
    pos = jnp.arange(s_, dtype=jnp.int32).reshape(nb, Q_BLOCK)
    key_pos = jnp.arange(s_, dtype=jnp.int32)

    def block(args):
        qi, Fi, pi = args
        s = jnp.einsum('bhqd,bhkd->bhqk', qi, k).astype(jnp.float32) * scale
        s = s + Fi[..., None] - F[:, :, None, :]
        mask = pi[:, None] >= key_pos[None, :]
        s = jnp.where(mask, s, -jnp.inf)
        p = jax.nn.softmax(s, axis=-1)
        return jnp.einsum('bhqk,bhkd->bhqd', p.astype(v.dtype), v)

    out = lax.map(block, (qb, Fb, pos))
    return out.transpose(1, 2, 0, 3, 4).reshape(b_, h_, s_, d_)


def hgrn2_chunkwise(q, f_logit, i, lb):
    b_, s_, _ = q.shape
    n_c = s_ // CHUNK

    def heads(t, d):
        return t.astype(jnp.float32).reshape(b_, n_c, CHUNK, HGRN_HEADS, d).transpose(1, 0, 3, 2, 4)

    lbh = lb.astype(jnp.float32).reshape(HGRN_HEADS, 1, HGRN_KEY_DIM)
    f = lbh + (1.0 - lbh) * jax.nn.sigmoid(heads(f_logit, HGRN_KEY_DIM))
    kk = 1.0 - f
    g = jnp.log(f)
    qq = jax.nn.silu(heads(q, HGRN_KEY_DIM))
    ii = heads(i, HGRN_VAL_DIM)
    causal = jnp.tril(jnp.ones((CHUNK, CHUNK), dtype=bool))

    def step(state, inp):
        qc, kc, ic, gc = inp
        bcum = jnp.cumsum(gc, axis=2)
        diff = bcum[:, :, :, None, :] - bcum[:, :, None, :, :]
        decay = jnp.exp(jnp.where(causal[:, :, None], diff, -jnp.inf))
        attn = jnp.einsum('bhtd,bhsd,bhtsd->bhts', qc, kc, decay)
        intra = jnp.einsum('bhts,bhsv->bhtv', attn, ic)
        inter = jnp.einsum('bhtd,bhdv->bhtv', qc * jnp.exp(bcum), state)
        b_last = bcum[:, :, -1]
        k_dec = kc * jnp.exp(b_last[:, :, None, :] - bcum)
        new_state = jnp.exp(b_last)[..., None] * state + jnp.einsum('bhsd,bhsv->bhdv', k_dec, ic)
        return new_state, intra + inter

    s0 = jnp.zeros((b_, HGRN_HEADS, HGRN_KEY_DIM, HGRN_VAL_DIM), jnp.float32)
    _, outs = lax.scan(step, s0, (qq, kk, ii, g))
    return outs.transpose(1, 0, 3, 2, 4).reshape(b_, s_, HGRN_HEADS, HGRN_VAL_DIM)


def memory_cross_attention(h, mem_n, w_q, w_kv, w_o):
    b_, s_, _ = h.shape
    m_ = mem_n.shape[1]
    q = (h @ w_q).reshape(b_, s_, X_HEADS, X_HEAD_DIM)
    kv = mem_n @ w_kv
    k, v = jnp.split(kv, 2, axis=-1)
    k = k.reshape(b_, m_, X_HEADS, X_HEAD_DIM)
    v = v.reshape(b_, m_, X_HEADS, X_HEAD_DIM)
    s = jnp.einsum('bshd,bmhd->bhsm', q, k).astype(jnp.float32) / math.sqrt(X_HEAD_DIM)
    p = jax.nn.softmax(s, axis=-1)
    o = jnp.einsum('bhsm,bmhd->bshd', p.astype(v.dtype), v).reshape(b_, s_, D_MODEL)
    return o @ w_o


def setup_inputs(seed: int = 0) -> dict:
    key = jax.random.key(seed)
    ks = jax.random.split(key, 20)
    nrm = lambda k, shape, s: jax.random.normal(k, shape, jnp.float32) * s
    gain = lambda k, shape: 1.0 + 0.02 * jax.random.normal(k, shape, jnp.float32)
    return {
        "x": nrm(ks[0], (BATCH, SEQ, D_MODEL), 1.0),
        "mem": nrm(ks[1], (BATCH, N_MEM, D_MODEL), 1.0),
        "norm_mix_g": gain(ks[2], (DEPTH, D_MODEL)),
        "w_in": nrm(ks[3], (DEPTH, D_MODEL, IN_COLS), D_MODEL ** -0.5),
        "fox_f_bias": 1.0 + 0.1 * jax.random.normal(ks[4], (DEPTH, FOX_HEADS), jnp.float32),
        "hgrn_lb_logits": nrm(ks[5], (DEPTH + 1, HGRN_WIDTH), 0.1),
        "hgrn_norm_g": gain(ks[6], (DEPTH, HGRN_VAL_DIM)),
        "w_out": nrm(ks[7], (DEPTH, MIX_WIDTH, D_MODEL), MIX_WIDTH ** -0.5),
        "norm_x_g": gain(ks[8], (DEPTH, D_MODEL)),
        "norm_mem_g": gain(ks[9], (DEPTH, D_MODEL)),
        "w_xq": nrm(ks[10], (DEPTH, D_MODEL, D_MODEL), D_MODEL ** -0.5),
        "w_xkv": nrm(ks[11], (DEPTH, D_MODEL, 2 * D_MODEL), D_MODEL ** -0.5),
        "w_xo": nrm(ks[12], (DEPTH, D_MODEL, D_MODEL), D_MODEL ** -0.5),
        "norm_ff_g": gain(ks[13], (DEPTH, D_MODEL)),
        "w1": nrm(ks[14], (DEPTH, D_MODEL, D_FF), D_MODEL ** -0.5),
        "w2": nrm(ks[15], (DEPTH, D_FF, D_MODEL), D_FF ** -0.5),
        "final_norm_g": gain(ks[16], (D_MODEL,)),
    }


def reference(x, mem, norm_mix_g, w_in, fox_f_bias, hgrn_lb_logits, hgrn_norm_g, w_out,
              norm_x_g, norm_mem_g, w_xq, w_xkv, w_xo, norm_ff_g, w1, w2, final_norm_g):
    b_, s_, _ = x.shape
    lbs = jnp.cumsum(jax.nn.softmax(hgrn_lb_logits.astype(jnp.float32), axis=0), axis=0)
    split_idx = list(np.cumsum(SPLITS)[:-1])
    h = x
    for l in range(DEPTH):
        hn = rmsnorm(h, norm_mix_g[l])
        proj = hn @ w_in[l]
        fq, fk, fv, ff, gq, gf, gi, gg = jnp.split(proj, split_idx, axis=-1)
        to_heads = lambda t: t.reshape(b_, s_, FOX_HEADS, FOX_HEAD_DIM).transpose(0, 2, 1, 3)
        log_f = jax.nn.log_sigmoid((ff + fox_f_bias[l]).astype(jnp.float32)).transpose(0, 2, 1)
        fox_out = fox_attention(to_heads(fq), to_heads(fk), to_heads(fv), log_f)
        fox_out = fox_out.transpose(0, 2, 1, 3).reshape(b_, s_, FOX_WIDTH)
        rec = hgrn2_chunkwise(gq, gf, gi, lbs[l])
        rec = rmsnorm(rec, hgrn_norm_g[l]) * jax.nn.silu(
            gg.astype(jnp.float32).reshape(b_, s_, HGRN_HEADS, HGRN_VAL_DIM))
        rec = rec.reshape(b_, s_, HGRN_WIDTH).astype(h.dtype)
        h = h + jnp.concatenate([fox_out, rec], axis=-1) @ w_out[l]
        mem_n = rmsnorm(mem, norm_mem_g[l])
        h = h + memory_cross_attention(rmsnorm(h, norm_x_g[l]), mem_n, w_xq[l], w_xkv[l], w_xo[l])
        u = rmsnorm(h, norm_ff_g[l]) @ w1[l]
        h = h + jnp.square(jax.nn.relu(u)) @ w2[l]
    return rmsnorm(h, final_norm_g)
```

```python
import contextlib
import os
import numpy as np
import concourse.bass as bass
import concourse.mybir as mybir
from concourse.bass_utils import run_bass_kernel_spmd

F32 = mybir.dt.float32
BF16 = mybir.dt.bfloat16
AF = mybir.ActivationFunctionType
ALU = mybir.AluOpType

D = 1024
NT = 2048
NP = 2048
NB = 32
NOB = 16
DFF = 4096
EPS = 1e-6
NEG = -30000.0
C_FQ, C_FK, C_FV, C_FF, C_GQ, C_GF, C_GI, C_GG = 0, 512, 1024, 1536, 1544, 2056, 2568, 3080
K_ID, K_TRIF, K_TRI2, K_ONES, K_IND2, K_MASK, K_TRI2X4, K_W = 0, 128, 256, 384, 512, 514, 642, 1154


class Prog:
    ENGS = ("pe", "act", "dve", "pool", "sp")

    def __init__(self, nc):
        self.nc = nc
        self.ops = {e: [] for e in self.ENGS}
        self.sems = {}
        self.cnt = {}
        self.seen = {e: {} for e in self.ENGS}
        self.lastw = {}
        self.readers = {}
        self._stack = []
        for e in self.ENGS:
            self._mksem("E_" + e)

    def _mksem(self, key):
        cm = self.nc.semaphore(key)
        h = cm.__enter__()
        self._stack.append(cm)
        self.sems[key] = h
        self.cnt[key] = 0
        return h

    def close(self):
        for cm in reversed(self._stack):
            cm.__exit__(None, None, None)

    def _deps(self, eng, reads, writes):
        deps = {}

        def add(ev):
            if ev is None:
                return
            sk, v = ev
            if deps.get(sk, 0) < v:
                deps[sk] = v
        for k in reads:
            add(self.lastw.get(k))
        for k in writes:
            add(self.lastw.get(k))
            for r in self.readers.get(k, ()):
                add(r)
        waits = []
        for sk, v in deps.items():
            if eng == "pe" and sk == "E_pe":
                continue
            if self.seen[eng].get(sk, 0) >= v:
                continue
            self.seen[eng][sk] = v
            waits.append((self.sems[sk], v))
        return waits

    def _commit(self, ev, reads, writes):
        for k in writes:
            self.lastw[k] = ev
            self.readers[k] = []
        for k in reads:
            self.readers.setdefault(k, []).append(ev)

    def op(self, eng, fn, reads=(), writes=()):
        self.group(eng, [fn], reads, writes)

    def group(self, eng, fns, reads=(), writes=()):
        reads = list(reads)
        writes = list(writes)
        waits = self._deps(eng, reads, writes)
        sk = "E_" + eng
        self.cnt[sk] += 1
        ev = (sk, self.cnt[sk])
        sem = self.sems[sk]

        def emit(e, waits=waits, fns=fns, sem=sem):
            for s, v in waits:
                e.wait_ge(s, v)
            ins = None
            for f in fns:
                ins = f(e)
            ins.then_inc(sem, 1)
        self.ops[eng].append(emit)
        self._commit(ev, reads, writes)

    def dma(self, eng, out, in_, semkey=None, reads=(), writes=()):
        reads = list(reads)
        writes = list(writes)
        waits = self._deps(eng, reads, writes)
        if semkey is None:
            self._uniq = getattr(self, "_uniq", 0) + 1
            semkey = "d_u%d" % self._uniq
        if semkey not in self.sems:
            self._mksem(semkey)
        self.cnt[semkey] += 16
        ev = (semkey, self.cnt[semkey])
        sem = self.sems[semkey]

        def emit(e, waits=waits, sem=sem, out=out, in_=in_):
            for s, v in waits:
                e.wait_ge(s, v)
            e.dma_start(out=out, in_=in_).then_inc(sem, 16)
        self.ops[eng].append(emit)
        self._commit(ev, reads, writes)

    def barrier(self):
        for eng in self.ENGS:
            waits = []
            for sk, h in self.sems.items():
                v = self.cnt[sk]
                if v > self.seen[eng].get(sk, 0):
                    self.seen[eng][sk] = v
                    waits.append((h, v))

            def emit(e, waits=waits):
                for s, v in waits:
                    e.wait_ge(s, v)
            self.ops[eng].append(emit)

    def flush(self):
        nc = self.nc
        ops = self.ops
        self.ops = {e: [] for e in self.ENGS}
        with nc.Block() as block:
            @block.tensor
            def _(e):
                for f in ops["pe"]:
                    f(e)

            @block.scalar
            def _(e):
                for f in ops["act"]:
                    f(e)

            @block.vector
            def _(e):
                for f in ops["dve"]:
                    f(e)

            @block.gpsimd
            def _(e):
                for f in ops["pool"]:
                    f(e)

            @block.sync
            def _(e):
                for f in ops["sp"]:
                    f(e)


def build_program(debug=()):
    nc = bass.Bass("TRN2", target_bir_lowering=False)

    def din(name, shape):
        return nc.dram_tensor(name, list(shape), F32, kind="ExternalInput").ap()
    xo = din("xo", [NT, D])
    xp = din("xp", [NP, D])
    memd = din("mem", [256, D])
    vld_d = din("vld", [128, 1])
    consts_d = din("consts", [128, K_W])
    gmix_d = din("gmix", [128, D])
    gx_d = din("gx", [128, D])
    gmem_d = din("gmem", [128, D])
    gff_d = din("gff", [128, D])
    gfin_d = din("gfin", [128, D])
    gn4_d = din("gn4", [128, 512])
    gncol_d = din("gncol", [128, 1])
    fb32_d = din("fb32", [128, 256])
    lb0_d = din("lb0", [128, 512])
    lb1_d = din("lb1", [128, 512])
    w_in = din("w_in", [D, 3592])
    w_out = din("w_out", [D, D])
    w_xq = din("w_xq", [D, D])
    w_xkv = din("w_xkv", [D, 2 * D])
    w_xo = din("w_xo", [D, D])
    w1 = din("w1", [D, DFF])
    w2 = din("w2", [DFF, D])
    out_d = nc.dram_tensor("out", [NT, D], F32, kind="ExternalOutput").ap()
    dbg_d = {}

    def wview(w):
        return w.rearrange("(k p) n -> p k n", p=128)
    w_in_v, w_out_v, w_xq_v, w_xkv_v, w_xo_v, w1_v, w2_v = map(wview, (w_in, w_out, w_xq, w_xkv, w_xo, w1, w2))

    P = Prog(nc)
    es0 = contextlib.ExitStack()

    def sb(es, name, shape, dt):
        return es.enter_context(nc.sbuf_tensor("s_" + name, list(shape), dt))

    psw = [es0.enter_context(nc.psum_tensor("psw%d" % i, [128, 1024], F32)) for i in range(4)]
    ps = [psw[i // 2][:, (i % 2) * 512:(i % 2 + 1) * 512] for i in range(8)]

    def psb(i):
        return ps[i][:].bitcast(BF16)

    cF = sb(es0, "cF", [128, K_W], F32)
    cB = sb(es0, "cB", [128, K_W], BF16)
    mixT = sb(es0, "mixT", [128, 8, NT], BF16)
    vld = sb(es0, "vld", [128, 1], F32)
    stat = sb(es0, "stat", [128, 64], F32)
    kxT = sb(es0, "kxT", [128, 8, 256], BF16)
    vx = sb(es0, "vx", [128, 2, D], BF16)
    P.dma("sp", cF[:], consts_d, writes=["cF"])
    P.dma("pool", cB[:], consts_d, writes=["cB"])
    P.dma("sp", vld[:], vld_d, writes=["vld"])
    identB = cB[:, K_ID:K_ID + 128]
    onesB = cB[:, K_ONES:K_ONES + 128]
    maskB = cB[:, K_MASK:K_MASK + 128]
    identF = cF[:, K_ID:K_ID + 128]
    triF = cF[:, K_TRIF:K_TRIF + 128]
    tri2 = cF[:, K_TRI2:K_TRI2 + 128]
    onesF = cF[:, K_ONES:K_ONES + 128]
    ind2 = cF[:, K_IND2:K_IND2 + 2]
    tri2x4 = cF[:, K_TRI2X4:K_TRI2X4 + 512]

    def dump(name, ap, shape, dt, reads=()):
        if name in debug:
            P.barrier()
            t = nc.dram_tensor("dbg_" + name, list(shape), dt, kind="ExternalOutput").ap()
            dbg_d[name] = t
            P.dma("sp", t, ap, "d_dbg", reads=list(reads))

    def norm_stats(src_ap, src_keys, junk, jkey, scol):
        k0, k1, k2 = "st%d" % scol, "st%d" % (scol + 1), "st%d" % (scol + 2)
        P.op("act", lambda e: e.activation(junk[:], src_ap, AF.Square, accum_out=stat[:, scol:scol + 1]),
             reads=src_keys, writes=[jkey, k0])
        P.op("act", lambda e: e.activation(stat[:, scol + 1:scol + 2], stat[:, scol:scol + 1], AF.Ln, bias=EPS, scale=1.0 / D),
             reads=[k0], writes=[k1])
        P.op("act", lambda e: e.activation(stat[:, scol + 2:scol + 3], stat[:, scol + 1:scol + 2], AF.Exp, scale=-0.5),
             reads=[k1], writes=[k2])

    def norm_tail(src_ap, src_keys, gain_ap, gain_key, hb, hbkey, dstT, dst_keys, psbank, scol, copy_eng):
        k2 = "st%d" % (scol + 2)
        P.op("dve", lambda e: e.scalar_tensor_tensor(hb[:], src_ap, stat[:, scol + 2:scol + 3], gain_ap, ALU.mult, ALU.mult),
             reads=src_keys + [k2, gain_key], writes=[hbkey])
        pT = psb(psbank)
        P.group("pe", [(lambda e, c=c: e.transpose(pT[:, c * 128:(c + 1) * 128], hb[:, c * 128:(c + 1) * 128], identB)) for c in range(8)],
                reads=[hbkey, "cB"], writes=["ps%d" % psbank])
        src = pT.rearrange("p (c t) -> p c t", c=8)
        if copy_eng == "act":
            P.op("act", lambda e: e.activation(dstT, src, AF.Copy), reads=["ps%d" % psbank], writes=dst_keys)
        else:
            P.op("dve", lambda e: e.tensor_copy(dstT, src), reads=["ps%d" % psbank], writes=dst_keys)

    def norm_to_T(src_ap, src_keys, gain_ap, gain_key, hb, hbkey, junk, jkey, dstT, dst_keys, psbank, scol, copy_eng):
        norm_stats(src_ap, src_keys, junk, jkey, scol)
        norm_tail(src_ap, src_keys, gain_ap, gain_key, hb, hbkey, dstT, dst_keys, psbank, scol, copy_eng)

    es1 = contextlib.ExitStack()
    hnT = sb(es1, "hnT", [128, 8, NB * 128], BF16)
    esAB = contextlib.ExitStack()
    whg = sb(esAB, "whg", [128, 8, 2048], BF16)
    for j in range(4):
        P.dma("pool", whg[:, :, j * 512:(j + 1) * 512], w_in_v[:, :, C_GQ + j * 512:C_GQ + (j + 1) * 512], "d_whg", writes=["whg"])

    with contextlib.ExitStack() as es:
        xt = [sb(es, "xt%d" % i, [128, D], F32) for i in range(4)]
        gmix = sb(es, "gmix", [128, D], F32)
        hbA = [sb(es, "hbA%d" % i, [128, D], BF16) for i in range(2)]
        junkA = sb(es, "junkA", [128, D], BF16)
        P.dma("sp", gmix[:], gmix_d, writes=["gmix"])

        wkv = [sb(es, "wkv%d" % i, [128, 8, 512], BF16) for i in range(2)]
        memT = sb(es, "memT", [128, 8, 256], BF16)
        mt = [sb(es, "mt%d" % i, [128, D], F32) for i in range(2)]
        gmem = sb(es, "gmem", [128, D], F32)
        hbM = [sb(es, "hbM%d" % i, [128, D], BF16) for i in range(2)]
        junkM = sb(es, "junkM", [128, D], BF16)
        P.dma("sp", gmem[:], gmem_d, writes=["gmem"])

        def mem_block(mb):
            P.dma("sp", mt[mb][:], memd[mb * 128:(mb + 1) * 128, :], writes=["mt%d" % mb])
            norm_to_T(mt[mb][:], ["mt%d" % mb], gmem[:], "gmem", hbM[mb], "hbM%d" % mb, junkM, "junkM",
                      memT[:, :, mb * 128:(mb + 1) * 128], ["memT%d" % mb], 4 + mb, 24 + mb * 3, "dve")

        def kv_part(part):
            wb = wkv[part % 2]
            wk = "wkv%d" % (part % 2)
            P.dma("pool", wb[:], w_xkv_v[:, :, part * 512:(part + 1) * 512], "d_" + wk, writes=[wk])
            if part < 2:
                def kx(jj):
                    j = part * 4 + jj
                    bank = jj % 2
                    P.group("pe", [(lambda e, c=c: e.matmul(ps[bank][:, 0:256], wb[:, c, jj * 128:(jj + 1) * 128], memT[:, c, :], start=(c == 0), stop=(c == 7)))
                                   for c in range(8)], reads=["memT0", "memT1", wk], writes=["ps%d" % bank])
                    P.op("dve", lambda e: e.tensor_copy(kxT[:, j, :], ps[bank][:, 0:256]), reads=["ps%d" % bank], writes=["kxT"])
                for jj in range(4):
                    kx(jj)
            else:
                n = part - 2

                def vxb(mb):
                    bank = 2 + mb
                    P.group("pe", [(lambda e, c=c: e.matmul(ps[bank][:], memT[:, c, mb * 128:(mb + 1) * 128], wb[:, c, :], start=(c == 0), stop=(c == 7)))
                                   for c in range(8)], reads=["memT%d" % mb, wk], writes=["ps%d" % bank])
                    P.op("act", lambda e: e.activation(vx[:, mb, n * 512:(n + 1) * 512], ps[bank][:], AF.Copy), reads=["ps%d" % bank], writes=["vx"])
                vxb(0)
                vxb(1)
        def phaseA_load(b):
            src = xp[b * 128:(b + 1) * 128, :] if b < 16 else xo[(b - 16) * 128:(b - 15) * 128, :]
            s = b % 4
            P.dma("sp", xt[s][:], src, "d_xt%d" % s, writes=["xt%d" % s])
            norm_stats(xt[s][:], ["xt%d" % s], junkA, "junkA", (b % 8) * 3)

        def phaseA(b):
            s = b % 4
            norm_tail(xt[s][:], ["xt%d" % s], gmix[:], "gmix", hbA[b % 2], "hbA%d" % (b % 2),
                      hnT[:, :, b * 128:(b + 1) * 128], ["hnT%d" % b], b % 2, (b % 8) * 3,
                      "act" if b % 2 else "dve")
        phaseA_load(0)
        phaseA_load(1)
        for b in range(NB):
            if b + 2 < NB:
                phaseA_load(b + 2)
            phaseA(b)
            if b == 5:
                mem_block(0)
            if b == 7:
                mem_block(1)
            if b in (10, 14, 18, 22):
                kv_part((b - 10) // 4)
        dump("hnT", hnT[:], [128, 8, NB * 128], BF16, ["hnT%d" % b for b in range(NB)])
        P.barrier()
        P.flush()

    with contextlib.ExitStack() as es:
        NS = 3
        lb = sb(es, "lb", [128, 512], F32)
        oml = sb(es, "oml", [128, 512], F32)
        tmpl = sb(es, "tmpl", [128, 512], F32)
        gncol = sb(es, "gncol", [128, 1], F32)
        S32 = sb(es, "S32", [128, 4, 128], F32)
        S16 = [sb(es, "S16_%d" % i, [128, 4, 128], BF16) for i in range(2)]
        stmp = sb(es, "stmp", [128, 4, 128], F32)
        Ft = [sb(es, "Ft%d" % i, [128, 512], F32) for i in range(NS)]
        Gt = [sb(es, "Gt%d" % i, [128, 512], F32) for i in range(NS)]
        Em = [sb(es, "Em%d" % i, [128, 512], F32) for i in range(NS)]
        R2 = [sb(es, "R2%d" % i, [128, 512], F32) for i in range(NS)]
        R3 = [sb(es, "R3%d" % i, [128, 512], F32) for i in range(NS)]
        EB = [sb(es, "EB%d" % i, [128, 8], F32) for i in range(NS)]
        KT = [sb(es, "KT%d" % i, [128, 512], BF16) for i in range(NS)]
        QT = [sb(es, "QT%d" % i, [128, 512], BF16) for i in range(NS)]
        GI = [sb(es, "GI%d" % i, [128, 512], BF16) for i in range(NS)]
        REC = [sb(es, "REC%d" % i, [128, 512], BF16) for i in range(NS)]
        KTT = [sb(es, "KTT%d" % i, [128, 512], BF16) for i in range(NS)]
        QTT = [sb(es, "QTT%d" % i, [128, 512], BF16) for i in range(NS)]
        AT = [sb(es, "AT%d" % i, [128, 512], BF16) for i in range(NS)]
        junkB = sb(es, "junkB", [128, 128], F32)
        rs4 = [sb(es, "rs4%d" % i, [128, 12], F32) for i in range(NS)]
        P.dma("sp", lb[:], lb0_d, writes=["lb"])
        P.dma("sp", tmpl[:], lb1_d, writes=["tmpl"])
        P.dma("sp", gncol[:], gncol_d, writes=["gncol"])
        P.op("dve", lambda e: e.tensor_tensor(tmpl[:], tmpl[:], lb[:], ALU.subtract), reads=["tmpl", "lb"], writes=["tmpl"])
        P.op("act", lambda e: e.activation(tmpl[:], tmpl[:], AF.Exp), reads=["tmpl"], writes=["tmpl"])
        P.op("dve", lambda e: e.tensor_scalar(tmpl[:], tmpl[:], 1.0, None, ALU.add), reads=["tmpl"], writes=["tmpl"])
        P.op("dve", lambda e: e.reciprocal(lb[:], tmpl[:]), reads=["tmpl"], writes=["lb"])
        P.op("dve", lambda e: e.tensor_scalar(oml[:], lb[:], -1.0, 1.0, ALU.mult, ALU.add), reads=["lb"], writes=["oml"])
        P.op("pool", lambda e: e.memset(S32[:], 0.0), writes=["S32"])
        P.op("pool", lambda e: e.memset(S16[0][:], 0.0), writes=["S16_0"])
        BG, BI, BQ, BC, BA, BS = 0, 1, 2, 3, 4, 5

        def proj_tok(bank, b, col0):
            P.group("pe", [(lambda e, c=c: e.matmul(ps[bank][:], hnT[:, c, b * 128:(b + 1) * 128], whg[:, c, col0:col0 + 512],
                                                      start=(c == 0), stop=(c == 7))) for c in range(8)],
                    reads=["hnT%d" % b, "whg"], writes=["ps%d" % bank])

        def front(b):
            own = b >= 16
            s = b % NS
            ft, gt, em, r2, r3, eb = Ft[s], Gt[s], Em[s], R2[s], R3[s], EB[s]
            kt, qt, gi, ktt, qtt, at = KT[s], QT[s], GI[s], KTT[s], QTT[s], AT[s]
            kf, kg, ke, k2, k3, keb = "Ft%d" % s, "Gt%d" % s, "Em%d" % s, "R2%d" % s, "R3%d" % s, "EB%d" % s
            kkt, kqt, kgi, kktt, kqtt, kat = "KT%d" % s, "QT%d" % s, "GI%d" % s, "KTT%d" % s, "QTT%d" % s, "AT%d" % s
            proj_tok(BG, b, 512); yield
            proj_tok(BI, b, 1024); yield
            P.op("act", lambda e: e.activation(ft[:], ps[BG][:], AF.Exp, scale=-1.0), reads=["ps%d" % BG], writes=[kf]); yield
            P.op("act", lambda e: e.activation(gi[:], ps[BI][:], AF.Copy), reads=["ps%d" % BI], writes=[kgi]); yield
            if own:
                proj_tok(BQ, b, 0); yield
            P.op("dve", lambda e: e.tensor_scalar(ft[:], ft[:], 1.0, None, ALU.add), reads=[kf], writes=[kf]); yield
            P.op("dve", lambda e: e.reciprocal(ft[:], ft[:]), reads=[kf], writes=[kf]); yield
            P.op("dve", lambda e: e.scalar_tensor_tensor(ft[:], ft[:], 1.0, oml[:], ALU.subtract, ALU.mult), reads=[kf, "oml"], writes=[kf]); yield
            P.op("act", lambda e: e.activation(gt[:], ft[:], AF.Ln, bias=1.0), reads=[kf], writes=[kg]); yield
            if own:
                P.op("act", lambda e: e.activation(r2[:], ps[BQ][:], AF.Exp, scale=-1.0), reads=["ps%d" % BQ], writes=[k2]); yield
            P.op("pe", lambda e: e.matmul(ps[BC][:], tri2, gt[:], start=True, stop=True), reads=[kg, "cF"], writes=["ps%d" % BC]); yield
            P.group("pe", [(lambda e, h=h: e.matmul(ps[BA][:, h * 2:h * 2 + 2], gt[:, h * 128:(h + 1) * 128], ind2, start=True, stop=True))
                           for h in range(4)], reads=[kg, "cF"], writes=["ps%d" % BA]); yield
            if own:
                proj_tok(BG, b, 1536); yield
            P.op("act", lambda e: e.activation(em[:], ps[BC][:], AF.Exp, scale=-1.0), reads=["ps%d" % BC], writes=[ke]); yield
            P.op("act", lambda e: e.activation(eb[:], ps[BA][:, 0:8], AF.Exp), reads=["ps%d" % BA], writes=[keb]); yield
            P.op("dve", lambda e: e.scalar_tensor_tensor(kt[:], ft[:], -1.0, em[:], ALU.mult, ALU.mult), reads=[kf, ke], writes=[kkt]); yield
            if own:
                P.op("act", lambda e: e.activation(r2[:], r2[:], AF.Ln, bias=1.0), reads=[k2], writes=[k2]); yield
                P.op("dve", lambda e: e.tensor_tensor(r2[:], ps[BC][:], r2[:], ALU.subtract), reads=[k2, "ps%d" % BC], writes=[k2]); yield
                P.op("act", lambda e: e.activation(r2[:], r2[:], AF.Exp), reads=[k2], writes=[k2]); yield
                P.op("dve", lambda e: e.tensor_tensor(qt[:], ps[BQ][:], r2[:], ALU.mult), reads=[k2, "ps%d" % BQ], writes=[kqt]); yield
                P.op("act", lambda e: e.activation(r3[:], ps[BG][:], AF.Exp, scale=-1.0), reads=["ps%d" % BG], writes=[k3]); yield
                P.op("dve", lambda e: e.tensor_scalar(r3[:], r3[:], 1.0, None, ALU.add), reads=[k3], writes=[k3]); yield
                P.op("dve", lambda e: e.reciprocal(r3[:], r3[:]), reads=[k3], writes=[k3]); yield
                P.op("dve", lambda e: e.tensor_tensor(r3[:], ps[BG][:], r3[:], ALU.mult), reads=[k3, "ps%d" % BG], writes=[k3]); yield
                pq = psb(BI)
                P.group("pe", [(lambda e, h=h: e.transpose(pq[:, h * 128:(h + 1) * 128], qt[:, h * 128:(h + 1) * 128], identB)) for h in range(4)] +
                              [(lambda e, h=h: e.transpose(pq[:, 512 + h * 128:512 + (h + 1) * 128], kt[:, h * 128:(h + 1) * 128], identB)) for h in range(4)],
                        reads=[kqt, kkt, "cB"], writes=["ps%d" % BI]); yield
                P.op("act", lambda e: e.activation(qtt[:], pq[:, 0:512], AF.Copy), reads=["ps%d" % BI], writes=[kqtt]); yield
                P.op("dve", lambda e: e.tensor_copy(ktt[:], pq[:, 512:1024]), reads=["ps%d" % BI], writes=[kktt]); yield
                P.group("pe", [(lambda e, h=h: e.matmul(ps[BA][:, h * 128:(h + 1) * 128], ktt[:, h * 128:(h + 1) * 128], qtt[:, h * 128:(h + 1) * 128],
                                                         start=True, stop=True)) for h in range(4)],
                        reads=[kktt, kqtt], writes=["ps%d" % BA]); yield
                P.op("dve", lambda e: e.tensor_tensor(at[:], ps[BA][:], tri2x4, ALU.mult), reads=["ps%d" % BA, "cF"], writes=[kat]); yield

        sidx = [0]

        def state(b):
            own = b >= 16
            s = b % NS
            kt, gi, eb, qtt, at = KT[s], GI[s], EB[s], QTT[s], AT[s]
            kkt, kgi, keb, kqtt, kat = "KT%d" % s, "GI%d" % s, "EB%d" % s, "QTT%d" % s, "AT%d" % s
            bo = 6 + b % 2
            s0 = sidx[0]
            for c in range(2):
                cur = sidx[0]
                nxt = 1 - cur
                P.group("pe", [(lambda e, h=h, c=c: e.matmul(ps[BS][:, h * 128:(h + 1) * 128], kt[c * 64:(c + 1) * 64, h * 128:(h + 1) * 128],
                                                              gi[c * 64:(c + 1) * 64, h * 128:(h + 1) * 128], start=True, stop=True)) for h in range(4)],
                        reads=[kkt, kgi], writes=["ps%d" % BS]); yield
                P.op("dve", lambda e: e.tensor_tensor(stmp[:], ps[BS][:].rearrange("p (h v) -> p h v", h=4), S32[:], ALU.add),
                     reads=["ps%d" % BS, "S32"], writes=["stmp"]); yield
                ebb = eb[:, c:8:2].unsqueeze(2).to_broadcast([128, 4, 128])
                P.op("dve", lambda e, ebb=ebb: e.tensor_tensor(S32[:], stmp[:], ebb, ALU.mult), reads=["stmp", keb], writes=["S32"]); yield
                P.op("act", lambda e, nxt=nxt: e.activation(S16[nxt][:], S32[:], AF.Copy), reads=["S32"], writes=["S16_%d" % nxt]); yield
                sidx[0] = nxt
                if own and c == 0:
                    fns = []
                    for h in range(4):
                        fns.append(lambda e, h=h: e.matmul(ps[bo][:, h * 128:(h + 1) * 128], at[:, h * 128:(h + 1) * 128], gi[:, h * 128:(h + 1) * 128],
                                                            start=True, stop=False))
                        fns.append(lambda e, h=h: e.matmul(ps[bo][0:64, h * 128:(h + 1) * 128], qtt[:, h * 128:h * 128 + 64], S16[s0][:, h, :],
                                                            start=False, stop=True))
                        fns.append(lambda e, h=h, nxt=nxt: e.matmul(ps[bo][64:128, h * 128:(h + 1) * 128], qtt[:, h * 128 + 64:(h + 1) * 128], S16[nxt][:, h, :],
                                                                     start=False, stop=True, tile_position=(0, 64)))
                    P.group("pe", fns, reads=[kat, kgi, kqtt, "S16_%d" % s0, "S16_%d" % nxt], writes=["ps%d" % bo]); yield

        def output(b):
            s = b % NS
            ob = b - 16
            bo = 6 + b % 2
            r3, rec, rs = R3[s], REC[s], rs4[s]
            k3, krec, krs = "R3%d" % s, "REC%d" % s, "rs4%d" % s
            for h in range(4):
                P.op("act", lambda e, h=h: e.activation(junkB[:], ps[bo][:, h * 128:(h + 1) * 128], AF.Square, accum_out=rs[:, h:h + 1]),
                     reads=["ps%d" % bo], writes=["junkB", krs + "a%d" % h]); yield
            P.op("act", lambda e: e.activation(rs[:, 4:8], rs[:, 0:4], AF.Ln, bias=EPS, scale=1.0 / 128), reads=[krs + "a%d" % h for h in range(4)], writes=[krs + "b"]); yield
            P.op("act", lambda e: e.activation(rs[:, 8:12], rs[:, 4:8], AF.Exp, scale=-0.5), reads=[krs + "b"], writes=[krs + "c"]); yield
            for h in range(4):
                P.op("dve", lambda e, h=h: e.scalar_tensor_tensor(rec[:, h * 128:(h + 1) * 128], ps[bo][:, h * 128:(h + 1) * 128], rs[:, 8 + h:9 + h],
                                                                  r3[:, h * 128:(h + 1) * 128], ALU.mult, ALU.mult),
                     reads=["ps%d" % bo, krs + "c", k3], writes=[krec + "_%d" % h]); yield
            pr = psb(bo)
            P.group("pe", [(lambda e, h=h: e.transpose(pr[:, h * 128:(h + 1) * 128], rec[:, h * 128:(h + 1) * 128], identB)) for h in range(4)],
                    reads=[krec + "_%d" % h for h in range(4)] + ["cB"], writes=["ps%d" % bo]); yield
            P.op("act", lambda e: e.activation(mixT[:, 4:8, ob * 128:(ob + 1) * 128], pr[:, 0:512].rearrange("p (h t) -> p h t", h=4), AF.Copy, scale=gncol[:, 0:1]),
                 reads=["ps%d" % bo, "gncol"], writes=["mixH%d" % ob]); yield

        for step in range(NB + 2):
            streams = []
            if step < NB:
                streams.append(front(step))
            if 0 <= step - 1 < NB:
                streams.append(state(step - 1))
            if 16 <= step - 2 < NB:
                streams.append(output(step - 2))
            while streams:
                for g_ in list(streams):
                    try:
                        next(g_)
                    except StopIteration:
                        streams.remove(g_)
        dump("mixH", mixT[:, 4:8, :], [128, 4, NT], BF16)
        P.barrier()
        P.flush()

    esAB.close()

    with contextlib.ExitStack() as es:
        Vp = sb(es, "Vp", [128, NB, 8, 65], BF16)
        NEGF = sb(es, "NEGF", [128, 256], F32)
        FTb = sb(es, "FTb", [8, NT], BF16)
        KTh = [sb(es, "KTh%d" % i, [65, NB * 128], BF16) for i in range(2)]
        QTh = [sb(es, "QTh%d" % i, [65, NT], BF16) for i in range(2)]
        onesrow = sb(es, "onesrow", [65, 64], F32)
        esC0 = contextlib.ExitStack()
        wv = sb(esC0, "wv", [128, 8, 512], BF16)
        wff = sb(esC0, "wff", [128, 8, 8], BF16)
        fb32 = sb(esC0, "fb32", [128, 256], F32)
        LF = sb(esC0, "LF", [128, 256], F32)
        CAR = sb(esC0, "CAR", [128, 256], F32)

        P.dma("pool", wv[:], w_in_v[:, :, C_FV:C_FV + 512], writes=["wv"])
        P.dma("pool", wff[:], w_in_v[:, :, C_FF:C_FF + 8], writes=["wff"])
        P.dma("sp", fb32[:], fb32_d, writes=["fb32"])
        P.op("pool", lambda e: e.memset(Vp[:, :, :, 64:65], 1.0), writes=["Vones"])
        P.op("pool", lambda e: e.tensor_scalar(Vp[:, 0:16, :, 64:65], Vp[:, 0:16, :, 64:65], vld[:, 0:1], None, ALU.mult),
             reads=["Vones", "vld"], writes=["Vones"])
        P.op("pool", lambda e: e.memset(onesrow[:], 1.0), writes=["onesrow"])
        P.op("pool", lambda e: e.memset(KTh[0][64:65, :], 1.0), writes=["KTones0"])
        P.op("pool", lambda e: e.memset(KTh[1][64:65, :], 1.0), writes=["KTones1"])

        def v_block(b):
            bank = b % 2
            P.group("pe", [(lambda e, c=c: e.matmul(ps[bank][:], hnT[:, c, b * 128:(b + 1) * 128], wv[:, c, :], start=(c == 0), stop=(c == 7)))
                           for c in range(8)], reads=["hnT%d" % b, "wv"], writes=["ps%d" % bank])
            dst = Vp[:, b, :, 0:64]
            src = ps[bank][:].rearrange("p (h d) -> p h d", h=8)
            if b % 2 == 0:
                P.op("act", lambda e: e.activation(dst, src, AF.Copy), reads=["ps%d" % bank], writes=["V%d" % b])
            else:
                P.op("dve", lambda e: e.tensor_copy(dst, src), reads=["ps%d" % bank], writes=["V%d" % b])
            P.group("pe", [(lambda e, c=c: e.matmul(ps[2][:, b * 8:(b + 1) * 8], hnT[:, c, b * 128:(b + 1) * 128], wff[:, c, :], start=(c == 0), stop=(c == 7)))
                           for c in range(8)], reads=["hnT%d" % b, "wff"], writes=["ps2"])
        for b in range(NB):
            v_block(b)
        P.op("dve", lambda e: e.tensor_tensor(LF[:], ps[2][:, 0:256], fb32[:], ALU.add), reads=["ps2", "fb32"], writes=["LF"])
        P.op("act", lambda e: e.activation(LF[:], LF[:], AF.Exp, scale=-1.0), reads=["LF"], writes=["LF"])
        P.op("act", lambda e: e.activation(LF[:], LF[:], AF.Ln, bias=1.0), reads=["LF"], writes=["LF"])
        P.op("dve", lambda e: e.tensor_scalar(LF[:], LF[:], -1.0, None, ALU.mult), reads=["LF"], writes=["LF"])
        P.op("pe", lambda e: e.matmul(ps[3][:, 0:256], triF, LF[:], start=True, stop=True), reads=["LF", "cF"], writes=["ps3"])
        P.op("pe", lambda e: e.matmul(ps[4][:, 0:256], onesF, LF[:], start=True, stop=True), reads=["LF", "cF"], writes=["ps4"])
        P.op("dve", lambda e: e.memset(CAR[:, 0:8], 0.0), writes=["CAR"])

        def carry(b):
            P.op("dve", lambda e: e.tensor_tensor(CAR[:, b * 8:(b + 1) * 8], CAR[:, (b - 1) * 8:b * 8], ps[4][:, (b - 1) * 8:b * 8], ALU.add),
                 reads=["CAR", "ps4"], writes=["CAR"])
        for b in range(1, NB):
            carry(b)
        P.op("dve", lambda e: e.scalar_tensor_tensor(NEGF[:], ps[3][:, 0:256], -1.0, CAR[:], ALU.mult, ALU.subtract),
             reads=["ps3", "CAR"], writes=["NEGF"])
        P.op("dve", lambda e: e.tensor_scalar(LF[:], NEGF[:], -1.0, None, ALU.mult), reads=["NEGF", "LF"], writes=["LF"])

        def ft_block(ob):
            b = 16 + ob
            bank = 5 + (ob // 4) % 2
            P.op("pe", lambda e: e.matmul(ps[bank][0:8, (ob % 4) * 128:(ob % 4 + 1) * 128], LF[:, b * 8:(b + 1) * 8], identF, start=True, stop=True),
                 reads=["LF", "cF"], writes=["ps%d" % bank])
            if ob % 4 == 3:
                q = ob // 4
                P.op("dve", lambda e: e.tensor_copy(FTb[:, q * 512:(q + 1) * 512], ps[bank][0:8, :]), reads=["ps%d" % bank], writes=["FTb"])
        for ob in range(NOB):
            ft_block(ob)
        dump("negF", NEGF[:], [128, 256], F32, ["NEGF"])
        P.barrier()
        P.flush()
        esC0.close()
        wqk = sb(es, "wqk", [128, 8, 256], BF16)
        PT = [sb(es, "PT%d" % i, [128, 1024], BF16) for i in range(3)]
        rrow = sb(es, "rrow", [65, 512], F32)
        tmpO = sb(es, "tmpO", [64, 512], F32)
        stB = [sb(es, "stB%d" % i, [64, 512], BF16) for i in range(2)]

        scale = 0.125

        def kproj(i, t):
            bank = t % 2
            P.group("pe", [(lambda e, c=c: e.matmul(ps[bank][0:64, :], wqk[:, c, 128 + i * 64:128 + (i + 1) * 64], hnT[:, c, t * 512:(t + 1) * 512],
                                                     start=(c == 0), stop=(c == 7))) for c in range(8)],
                    reads=["hnT%d" % (4 * t + j) for j in range(4)] + ["wqk"], writes=["ps%d" % bank])
            P.op("dve", lambda e: e.tensor_copy(KTh[i][0:64, t * 512:(t + 1) * 512], ps[bank][0:64, :]),
                 reads=["ps%d" % bank], writes=["KT%d_%d" % (i, t)])

        def qproj(i, t):
            bank = t % 2
            P.group("pe", [(lambda e, c=c: e.matmul(ps[bank][0:64, :], wqk[:, c, i * 64:(i + 1) * 64], hnT[:, c, 2048 + t * 512:2048 + (t + 1) * 512],
                                                     start=(c == 0), stop=(c == 7))) for c in range(8)],
                    reads=["hnT%d" % (16 + 4 * t + j) for j in range(4)] + ["wqk"], writes=["ps%d" % bank])
            P.op("dve", lambda e: e.tensor_scalar(QTh[i][0:64, t * 512:(t + 1) * 512], ps[bank][0:64, :], scale, None, ALU.mult),
                 reads=["ps%d" % bank], writes=["QT%d_%d" % (i, t)])

        def pair(p):
            P.dma("pool", wqk[:, :, 0:128], w_in_v[:, :, C_FQ + p * 128:C_FQ + (p + 1) * 128], "d_wqk", writes=["wqk"])
            P.dma("pool", wqk[:, :, 128:256], w_in_v[:, :, C_FK + p * 128:C_FK + (p + 1) * 128], "d_wqk", writes=["wqk"])
            for i in range(2):
                h = 2 * p + i
                P.dma("sp", QTh[i][64:65, :], FTb[h:h + 1, :], "d_ft%d" % i, reads=["FTb"], writes=["QTa%d" % i])
                for t in range(8):
                    kproj(i, t)
                for t in range(4):
                    qproj(i, t)
            items = []
            for i in range(2):
                for qp in range(2):
                    qa, qb = 2 * qp, 2 * qp + 1
                    for kb in range(16 + 4 * qb + 4):
                        items.append((i, qa, qb, kb))

            def tiles_of(n):
                i, qa, qb, kb = items[n]
                res = []
                for off, q in ((0, qa), (512, qb)):
                    nk = 16 + 4 * q + 4
                    if kb < nk:
                        j = kb - (16 + 4 * q)
                        res.append((off, q, max(j, 0) * 128, j, nk))
                return res

            def emit_qk(n):
                i, qa, qb, kb = items[n]
                w = n % 3
                fns = []
                for off, q, c0, j, nk in tiles_of(n):
                    fns.append(lambda e, off=off, q=q, c0=c0, j=j: e.matmul(psw[w][:, off + c0:off + 512], KTh[i][0:65, kb * 128:(kb + 1) * 128],
                                                                              QTh[i][0:65, q * 512 + c0:(q + 1) * 512], start=True, stop=(j < 0)))
                    if j >= 0:
                        fns.append(lambda e, off=off, c0=c0: e.matmul(psw[w][:, off + c0:off + c0 + 128], identB, maskB, start=False, stop=True))
                P.group("pe", fns, reads=["KT%d_%d" % (i, kb // 4), "KTones%d" % i, "QT%d_%d" % (i, qa), "QT%d_%d" % (i, qb), "QTa%d" % i, "cB"],
                        writes=["ps%d" % (2 * w), "ps%d" % (2 * w + 1)])

            def emit_rest(n):
                i, qa, qb, kb = items[n]
                h = 2 * p + i
                w = n % 3
                pt = PT[w]
                ptk = "PT%d" % w
                tl = tiles_of(n)
                lo = tl[0][0] + tl[0][2]
                P.op("act", lambda e: e.activation(pt[:, lo:1024], psw[w][:, lo:1024], AF.Exp, bias=NEGF[:, kb * 8 + h:kb * 8 + h + 1]),
                     reads=["ps%d" % (2 * w), "ps%d" % (2 * w + 1), "NEGF"], writes=[ptk])
                for off, q, c0, j, nk in tl:
                    ob_ = 6 + (off // 512)
                    P.op("pe", lambda e, off=off, c0=c0, nk=nk, ob_=ob_: e.matmul(ps[ob_][0:65, c0:512], Vp[:, kb, h, :], pt[:, off + c0:off + 512],
                                                                                 start=(kb == 0), stop=(kb == nk - 1)),
                         reads=[ptk, "V%d" % kb, "Vones"], writes=["ps%d" % ob_])
                    if kb == nk - 1:
                        finish(i, q, ob_, (n + 2) % 3)

            def finish(i, qi, ob_, wfree):
                sbank = 2 * wfree
                P.op("dve", lambda e: e.reciprocal(rrow[64:65, :], ps[ob_][64:65, :]), reads=["ps%d" % ob_], writes=["rrow"])
                P.op("pe", lambda e: e.matmul(ps[sbank][0:64, :], onesrow[64:65, :], rrow[64:65, :], start=True, stop=True),
                     reads=["rrow", "onesrow"], writes=["ps%d" % sbank])
                P.op("dve", lambda e: e.tensor_copy(tmpO[:], ps[ob_][0:64, :]), reads=["ps%d" % ob_], writes=["tmpO"])
                if i == 0:
                    P.op("dve", lambda e: e.tensor_tensor(mixT[0:64, p, qi * 512:(qi + 1) * 512], tmpO[:], ps[sbank][0:64, :], ALU.mult),
                         reads=["tmpO", "ps%d" % sbank], writes=["mixFa%d_%d" % (p, qi)])
                else:
                    sbf = stB[qi % 2]
                    P.op("dve", lambda e: e.tensor_tensor(sbf[:], tmpO[:], ps[sbank][0:64, :], ALU.mult),
                         reads=["tmpO", "ps%d" % sbank], writes=["stB%d" % (qi % 2)])
                    P.dma("sp", mixT[64:128, p, qi * 512:(qi + 1) * 512], sbf[:], "d_stb%d" % (qi % 2), reads=["stB%d" % (qi % 2)], writes=["mixFb%d_%d" % (p, qi)])

            LA = 2
            for n in range(LA):
                emit_qk(n)
            for n in range(len(items)):
                emit_rest(n)
                if n + LA < len(items):
                    emit_qk(n + LA)
        for p in range(4):
            pair(p)
        dump("mixF", mixT[:, 0:4, :], [128, 4, NT], BF16)
        P.barrier()
        P.flush()
    es1.close()

    es2 = contextlib.ExitStack()
    y = sb(es2, "y", [128, NOB, D], F32)
    h2T = sb(es2, "h2T", [128, 8, NT], BF16)
    esD = contextlib.ExitStack()
    wq = sb(esD, "wq", [128, 8, D], BF16)
    wxo = sb(esD, "wxo", [128, 8, D], BF16)

    def add_proj(b, lhs_fn, lhs_keys, W, wkey, nchunks):
        def half(n):
            bank = (2 * b + n) % 4
            P.group("pe", [(lambda e, c=c: e.matmul(ps[bank][:], lhs_fn(c), W[:, c, n * 512:(n + 1) * 512], start=(c == 0), stop=(c == nchunks - 1)))
                           for c in range(nchunks)], reads=lhs_keys + [wkey], writes=["ps%d" % bank])
            P.op("dve", lambda e: e.tensor_tensor(y[:, b, n * 512:(n + 1) * 512], ps[bank][:], y[:, b, n * 512:(n + 1) * 512], ALU.add),
                 reads=["ps%d" % bank, "y%d" % b], writes=["y%d" % b])
        half(0)
        half(1)

    with contextlib.ExitStack() as es:
        wo = sb(es, "wo", [128, 8, D], BF16)
        gx = sb(es, "gx", [128, D], F32)
        hbD = [sb(es, "hbD%d" % i, [128, D], BF16) for i in range(2)]
        junkD = sb(es, "junkD", [128, D], BF16)
        for j in range(2):
            P.dma("pool", wo[:, :, j * 512:(j + 1) * 512], w_out_v[:, :, j * 512:(j + 1) * 512], "d_wo", writes=["wo"])
        P.dma("sp", gx[:], gx_d, writes=["gx"])
        for j in range(2):
            P.dma("pool", wq[:, :, j * 512:(j + 1) * 512], w_xq_v[:, :, j * 512:(j + 1) * 512], "d_wq", writes=["wq"])
        for j in range(2):
            P.dma("pool", wxo[:, :, j * 512:(j + 1) * 512], w_xo_v[:, :, j * 512:(j + 1) * 512], "d_wxo", writes=["wxo"])
        for b in range(NOB):
            P.dma("sp", y[:, b, :], xo[b * 128:(b + 1) * 128, :], "d_y%d" % b, writes=["y%d" % b])

        def d0_proj(b):
            keys = ["mixH%d" % b] + ["mixFa%d_%d" % (p, b // 4) for p in range(4)] + ["mixFb%d_%d" % (p, b // 4) for p in range(4)]
            add_proj(b, (lambda c: mixT[:, c, b * 128:(b + 1) * 128]), keys, wo, "wo", 8)
            norm_stats(y[:, b, :], ["y%d" % b], junkD, "junkD", (b % 8) * 3)

        def d0_norm(b):
            norm_tail(y[:, b, :], ["y%d" % b], gx[:], "gx", hbD[b % 2], "hbD%d" % (b % 2),
                      h2T[:, :, b * 128:(b + 1) * 128], ["h2T%d" % b], 4 + b % 2, (b % 8) * 3,
                      "act" if b % 2 else "dve")
        d0_proj(0)
        for b in range(NOB):
            if b + 1 < NOB:
                d0_proj(b + 1)
            d0_norm(b)
        dump("y1", y[:], [128, NOB, D], F32, ["y%d" % b for b in range(NOB)])
        P.barrier()
        P.flush()

    with contextlib.ExitStack() as es:
        gff = sb(es, "gff", [128, D], F32)
        P.dma("sp", gff[:], gff_d, writes=["gff"])
        qxT = sb(es, "qxT", [128, 8, 512], BF16)
        oxT = sb(es, "oxT", [128, 8, 512], BF16)
        PTx = [sb(es, "PTx%d" % i, [128, 2, 512], BF16) for i in range(2)]
        rden = sb(es, "rden", [128, 512], F32)
        hbE = [sb(es, "hbE%d" % i, [128, D], BF16) for i in range(2)]
        junkE = sb(es, "junkE", [128, D], BF16)

        def d1_tile(T):
            tk = ["h2T%d" % (4 * T + j) for j in range(4)]

            def qx(j):
                bank = j % 2
                P.group("pe", [(lambda e, c=c: e.matmul(ps[bank][:], wq[:, c, j * 128:(j + 1) * 128], h2T[:, c, T * 512:(T + 1) * 512], start=(c == 0), stop=(c == 7)))
                               for c in range(8)], reads=tk + ["wq"], writes=["ps%d" % bank])
                if j % 2 == 0:
                    P.op("act", lambda e: e.activation(qxT[:, j, :], ps[bank][:], AF.Copy, scale=1.0 / 16), reads=["ps%d" % bank], writes=["qxT%d" % j])
                else:
                    P.op("dve", lambda e: e.tensor_scalar(qxT[:, j, :], ps[bank][:], 1.0 / 16, None, ALU.mult), reads=["ps%d" % bank], writes=["qxT%d" % j])
            for j in range(8):
                qx(j)

            def head(hx, part):
                ptx = PTx[hx % 2]
                pk = "PTx%d" % (hx % 2)

                def sc(mb):
                    bank = 2 + mb
                    P.group("pe", [(lambda e, dc=dc: e.matmul(ps[bank][:], kxT[:, hx * 2 + dc, mb * 128:(mb + 1) * 128], qxT[:, hx * 2 + dc, :], start=(dc == 0), stop=(dc == 1)))
                                   for dc in range(2)], reads=["kxT", "qxT%d" % (hx * 2), "qxT%d" % (hx * 2 + 1)], writes=["ps%d" % bank])
                    P.op("act", lambda e: e.activation(ptx[:, mb, :], ps[bank][:], AF.Exp), reads=["ps%d" % bank], writes=[pk + "_%d" % mb])
                if part == 0:
                    sc(0)
                    sc(1)
                    return
                P.group("pe", [(lambda e, mb=mb: e.matmul(ps[4][:], onesB, ptx[:, mb, :], start=(mb == 0), stop=(mb == 1))) for mb in range(2)],
                        reads=[pk + "_0", pk + "_1", "cB"], writes=["ps4"])
                P.op("dve", lambda e: e.reciprocal(rden[:], ps[4][:]), reads=["ps4"], writes=["rden"])

                def ov(dc):
                    bank = 5 + dc
                    P.group("pe", [(lambda e, mb=mb: e.matmul(ps[bank][:], vx[:, mb, (hx * 2 + dc) * 128:(hx * 2 + dc + 1) * 128], ptx[:, mb, :], start=(mb == 0), stop=(mb == 1)))
                                   for mb in range(2)], reads=[pk + "_0", pk + "_1", "vx"], writes=["ps%d" % bank])
                    P.op("dve", lambda e: e.tensor_tensor(oxT[:, hx * 2 + dc, :], ps[bank][:], rden[:], ALU.mult),
                         reads=["ps%d" % bank, "rden"], writes=["oxT%d" % (hx * 2 + dc)])
                ov(0)
                ov(1)
            head(0, 0)
            for hx in range(4):
                if hx + 1 < 4:
                    head(hx + 1, 0)
                head(hx, 1)

            def blk_proj(bl):
                b = 4 * T + bl
                add_proj(b, (lambda c: oxT[:, c, bl * 128:(bl + 1) * 128]), ["oxT%d" % c for c in range(8)], wxo, "wxo", 8)
                norm_stats(y[:, b, :], ["y%d" % b], junkE, "junkE", (b % 8) * 3)

            def blk_norm(bl):
                b = 4 * T + bl
                norm_tail(y[:, b, :], ["y%d" % b], gff[:], "gff", hbE[b % 2], "hbE%d" % (b % 2),
                          h2T[:, :, b * 128:(b + 1) * 128], ["h2T%d" % b], 6 + b % 2, (b % 8) * 3,
                          "act" if b % 2 else "dve")
            blk_proj(0)
            for bl in range(4):
                if bl + 1 < 4:
                    blk_proj(bl + 1)
                blk_norm(bl)
        for T in range(4):
            d1_tile(T)
        dump("y2", y[:], [128, NOB, D], F32)
        P.barrier()
        P.flush()

    esD.close()

    with contextlib.ExitStack() as es:
        w1b = [sb(es, "w1b%d" % i, [128, 8, 512], BF16) for i in range(2)]
        w2b = [sb(es, "w2b%d" % i, [128, 4, D], BF16) for i in range(2)]
        U = [sb(es, "U%d" % i, [128, 4, 512], BF16) for i in range(2)]
        Rr = [sb(es, "Rr%d" % i, [128, 512], F32) for i in range(2)]
        gfin = sb(es, "gfin", [128, D], F32)
        ost = [sb(es, "ost%d" % i, [128, D], F32) for i in range(2)]
        junkF = sb(es, "junkF", [128, D], BF16)
        P.dma("sp", gfin[:], gfin_d, writes=["gfin"])
        NG = 8

        def ffn(g, T):
            wb1, wb2 = w1b[g % 2], w2b[g % 2]
            k1, k2 = "w1b%d" % (g % 2), "w2b%d" % (g % 2)
            u = U[T % 2]
            uk = "U%d" % (T % 2)
            tk = ["h2T%d" % (4 * T + j) for j in range(4)]

            def up(m):
                bank = m % 2
                rr = Rr[m % 2]
                rk = "Rr%d" % (m % 2)
                P.group("pe", [(lambda e, c=c: e.matmul(ps[bank][:], wb1[:, c, m * 128:(m + 1) * 128], h2T[:, c, T * 512:(T + 1) * 512], start=(c == 0), stop=(c == 7)))
                               for c in range(8)], reads=tk + [k1], writes=["ps%d" % bank])
                P.op("act", lambda e: e.activation(rr[:], ps[bank][:], AF.Relu), reads=["ps%d" % bank], writes=[rk])
                P.op("pool", lambda e: e.tensor_tensor(u[:, m, :], rr[:], rr[:], ALU.mult), reads=[rk], writes=[uk + "_%d" % m])
            for m in range(4):
                up(m)

            def down(bl, n):
                b = 4 * T + bl
                bank = 2 + (2 * bl + n) % 4
                P.group("pe", [(lambda e, m=m: e.matmul(ps[bank][:], u[:, m, bl * 128:(bl + 1) * 128], wb2[:, m, n * 512:(n + 1) * 512], start=(m == 0), stop=(m == 3)))
                               for m in range(4)], reads=[uk + "_%d" % m for m in range(4)] + [k2], writes=["ps%d" % bank])
                P.op("dve", lambda e: e.tensor_tensor(y[:, b, n * 512:(n + 1) * 512], ps[bank][:], y[:, b, n * 512:(n + 1) * 512], ALU.add),
                     reads=["ps%d" % bank, "y%d" % b], writes=["y%d" % b])

            def fin(b):
                sc = (b % 8) * 3
                o = ost[b % 2]
                k0, k1_, k2_ = "st%d" % sc, "st%d" % (sc + 1), "st%d" % (sc + 2)
                P.op("act", lambda e: e.activation(junkF[:], y[:, b, :], AF.Square, accum_out=stat[:, sc:sc + 1]),
                     reads=["y%d" % b], writes=["junkF", k0])
                P.op("act", lambda e: e.activation(stat[:, sc + 1:sc + 2], stat[:, sc:sc + 1], AF.Ln, bias=EPS, scale=1.0 / D),
                     reads=[k0], writes=[k1_])
                P.op("act", lambda e: e.activation(stat[:, sc + 2:sc + 3], stat[:, sc + 1:sc + 2], AF.Exp, scale=-0.5),
                     reads=[k1_], writes=[k2_])
                P.op("act", lambda e: e.activation(o[:], y[:, b, :], AF.Copy, scale=stat[:, sc + 2:sc + 3]),
                     reads=["y%d" % b, k2_], writes=["ost%d" % (b % 2)])
                P.op("pool", lambda e: e.tensor_tensor(o[:], o[:], gfin[:], ALU.mult),
                     reads=["ost%d" % (b % 2), "gfin"], writes=["ost%d" % (b % 2)])
                P.dma("sp", out_d[b * 128:(b + 1) * 128, :], o[:], "d_out%d" % (b % 2), reads=["ost%d" % (b % 2)])
            for bl in range(4):
                down(bl, 0)
                down(bl, 1)
                if g == NG - 1:
                    fin(4 * T + bl)

        for g in range(NG):
            P.dma("pool", w1b[g % 2][:], w1_v[:, :, g * 512:(g + 1) * 512], "d_w1_%d" % (g % 2), writes=["w1b%d" % (g % 2)])
            P.dma("pool", w2b[g % 2][:], w2_v[:, g * 4:(g + 1) * 4, :], "d_w2_%d" % (g % 2), writes=["w2b%d" % (g % 2)])
            for T in range(4):
                ffn(g, T)
        P.barrier()
        P.flush()
    es2.close()
    P.close()
    es0.close()
    return nc, dbg_d


def _consts():
    c = np.zeros((128, K_W), np.float32)
    s = np.arange(128)[:, None]
    t = np.arange(128)[None, :]
    c[:, K_ID:K_ID + 128] = np.eye(128)
    c[:, K_TRIF:K_TRIF + 128] = (s <= t)
    tri2 = ((s <= t) & (s // 64 == t // 64)).astype(np.float32)
    c[:, K_TRI2:K_TRI2 + 128] = tri2
    c[:, K_ONES:K_ONES + 128] = 1.0
    c[:, K_IND2:K_IND2 + 2] = (s // 64 == np.arange(2)[None, :])
    c[:, K_MASK:K_MASK + 128] = np.where(s <= t, 0.0, NEG)
    c[:, K_TRI2X4:K_TRI2X4 + 512] = np.tile(tri2, (1, 4))
    return c


def _bc(v, reps=1):
    v = np.asarray(v, np.float32).reshape(1, -1)
    return np.ascontiguousarray(np.broadcast_to(np.tile(v, (1, reps)), (128, v.shape[1] * reps)))


_CACHE = {}


def make_in_maps(x, mem, norm_mix_g, w_in, fox_f_bias, hgrn_lb_logits, hgrn_norm_g, w_out,
                 norm_x_g, norm_mem_g, w_xq, w_xkv, w_xo, norm_ff_g, w1, w2, final_norm_g):
    f = lambda a: np.ascontiguousarray(np.asarray(a, np.float32))
    x = f(x)
    mem = f(mem)
    shared = {
        "consts": _consts(),
        "gmix": _bc(norm_mix_g[0]), "gx": _bc(norm_x_g[0]), "gmem": _bc(norm_mem_g[0]),
        "gff": _bc(norm_ff_g[0]), "gfin": _bc(final_norm_g),
        "gn4": _bc(hgrn_norm_g[0], 4), "gncol": f(np.asarray(hgrn_norm_g[0]).reshape(128, 1)), "fb32": _bc(fox_f_bias[0], 32),
        "lb0": _bc(hgrn_lb_logits[0]), "lb1": _bc(hgrn_lb_logits[1]),
        "w_in": f(w_in[0]), "w_out": f(w_out[0]), "w_xq": f(w_xq[0]), "w_xkv": f(w_xkv[0]),
        "w_xo": f(w_xo[0]), "w1": f(w1[0]), "w2": f(w2[0]),
    }
    in_maps = []
    for c in range(8):
        b, half = c // 2, c % 2
        m = dict(shared)
        m["xo"] = np.ascontiguousarray(x[b, half * NT:(half + 1) * NT])
        m["xp"] = np.ascontiguousarray(x[b, 0:NP]) if half == 1 else np.zeros((NP, D), np.float32)
        m["vld"] = np.full((128, 1), float(half), np.float32)
        m["mem"] = np.ascontiguousarray(mem[b])
        in_maps.append(m)
    return in_maps


def kernel(**inputs):
    in_maps = make_in_maps(**inputs)
    if "nc" not in _CACHE:
        _CACHE["nc"] = build_program()[0]
    nc = _CACHE["nc"]
    res = run_bass_kernel_spmd(nc, in_maps, core_ids=list(range(8)))
    out = np.zeros((4, 4096, D), np.float32)
    for c in range(8):
        b, half = c // 2, c % 2
        out[b, half * NT:(half + 1) * NT] = np.asarray(res.results[c]["out"], np.float32)
    return out
```

```python
import contextlib
import os
import numpy as np
import concourse.bass as bass
import concourse.mybir as mybir
from concourse.bass_utils import run_bass_kernel_spmd

F32 = mybir.dt.float32
BF16 = mybir.dt.bfloat16
AF = mybir.ActivationFunctionType
ALU = mybir.AluOpType

D = 1024
NT = 2048
NP = 2048
NB = 32
NOB = 16
DFF = 4096
EPS = 1e-6
NEG = -30000.0
C_FQ, C_FK, C_FV, C_FF, C_GQ, C_GF, C_GI, C_GG = 0, 512, 1024, 1536, 1544, 2056, 2568, 3080
K_ID, K_TRIF, K_TRI2, K_ONES, K_IND2, K_MASK, K_TRI2X4, K_W = 0, 128, 256, 384, 512, 514, 642, 1154


class Prog:
    ENGS = ("pe", "act", "dve", "pool", "sp")

    def __init__(self, nc):
        self.nc = nc
        self.ops = {e: [] for e in self.ENGS}
        self.sems = {}
        self.cnt = {}
        self.seen = {e: {} for e in self.ENGS}
        self.lastw = {}
        self.readers = {}
        self._stack = []
        for e in self.ENGS:
            self._mksem("E_" + e)

    def _mksem(self, key):
        cm = self.nc.semaphore(key)
        h = cm.__enter__()
        self._stack.append(cm)
        self.sems[key] = h
        self.cnt[key] = 0
        return h

    def close(self):
        for cm in reversed(self._stack):
            cm.__exit__(None, None, None)

    def _deps(self, eng, reads, writes):
        deps = {}

        def add(ev):
            if ev is None:
                return
            sk, v = ev
            if deps.get(sk, 0) < v:
                deps[sk] = v
        for k in reads:
            add(self.lastw.get(k))
        for k in writes:
            add(self.lastw.get(k))
            for r in self.readers.get(k, ()):
                add(r)
        waits = []
        for sk, v in deps.items():
            if eng == "pe" and sk == "E_pe":
                continue
            if self.seen[eng].get(sk, 0) >= v:
                continue
            self.seen[eng][sk] = v
            waits.append((self.sems[sk], v))
        return waits

    def _commit(self, ev, reads, writes):
        for k in writes:
            self.lastw[k] = ev
            self.readers[k] = []
        for k in reads:
            self.readers.setdefault(k, []).append(ev)

    def op(self, eng, fn, reads=(), writes=()):
        self.group(eng, [fn], reads, writes)

    def group(self, eng, fns, reads=(), writes=()):
        reads = list(reads)
        writes = list(writes)
        waits = self._deps(eng, reads, writes)
        sk = "E_" + eng
        self.cnt[sk] += 1
        ev = (sk, self.cnt[sk])
        sem = self.sems[sk]

        def emit(e, waits=waits, fns=fns, sem=sem):
            for s, v in waits:
                e.wait_ge(s, v)
            ins = None
            for f in fns:
                ins = f(e)
            ins.then_inc(sem, 1)
        self.ops[eng].append(emit)
        self._commit(ev, reads, writes)

    def dma(self, eng, out, in_, semkey=None, reads=(), writes=()):
        reads = list(reads)
        writes = list(writes)
        waits = self._deps(eng, reads, writes)
        if semkey is None:
            self._uniq = getattr(self, "_uniq", 0) + 1
            semkey = "d_u%d" % self._uniq
        if semkey not in self.sems:
            self._mksem(semkey)
        self.cnt[semkey] += 16
        ev = (semkey, self.cnt[semkey])
        sem = self.sems[semkey]

        def emit(e, waits=waits, sem=sem, out=out, in_=in_):
            for s, v in waits:
                e.wait_ge(s, v)
            e.dma_start(out=out, in_=in_).then_inc(sem, 16)
        self.ops[eng].append(emit)
        self._commit(ev, reads, writes)

    def barrier(self):
        for eng in self.ENGS:
            waits = []
            for sk, h in self.sems.items():
                v = self.cnt[sk]
                if v > self.seen[eng].get(sk, 0):
                    self.seen[eng][sk] = v
                    waits.append((h, v))

            def emit(e, waits=waits):
                for s, v in waits:
                    e.wait_ge(s, v)
            self.ops[eng].append(emit)

    def flush(self):
        nc = self.nc
        ops = self.ops
        self.ops = {e: [] for e in self.ENGS}
        with nc.Block() as block:
            @block.tensor
            def _(e):
                for f in ops["pe"]:
                    f(e)

            @block.scalar
            def _(e):
                for f in ops["act"]:
                    f(e)

            @block.vector
            def _(e):
                for f in ops["dve"]:
                    f(e)

            @block.gpsimd
            def _(e):
                for f in ops["pool"]:
                    f(e)

            @block.sync
            def _(e):
                for f in ops["sp"]:
                    f(e)


def build_program(debug=()):
    nc = bass.Bass("TRN2", target_bir_lowering=False)

    def din(name, shape):
        return nc.dram_tensor(name, list(shape), F32, kind="ExternalInput").ap()
    xo = din("xo", [NT, D])
    xp = din("xp", [NP, D])
    memd = din("mem", [256, D])
    vld_d = din("vld", [128, 1])
    consts_d = din("consts", [128, K_W])
    gmix_d = din("gmix", [128, D])
    gx_d = din("gx", [128, D])
    gmem_d = din("gmem", [128, D])
    gff_d = din("gff", [128, D])
    gfin_d = din("gfin", [128, D])
    gn4_d = din("gn4", [128, 512])
    gncol_d = din("gncol", [128, 1])
    fb32_d = din("fb32", [128, 256])
    lb0_d = din("lb0", [128, 512])
    lb1_d = din("lb1", [128, 512])
    w_in = din("w_in", [D, 3592])
    w_out = din("w_out", [D, D])
    w_xq = din("w_xq", [D, D])
    w_xkv = din("w_xkv", [D, 2 * D])
    w_xo = din("w_xo", [D, D])
    w1 = din("w1", [D, DFF])
    w2 = din("w2", [DFF, D])
    out_d = nc.dram_tensor("out", [NT, D], F32, kind="ExternalOutput").ap()
    dbg_d = {}

    def wview(w):
        return w.rearrange("(k p) n -> p k n", p=128)
    w_in_v, w_out_v, w_xq_v, w_xkv_v, w_xo_v, w1_v, w2_v = map(wview, (w_in, w_out, w_xq, w_xkv, w_xo, w1, w2))

    P = Prog(nc)
    es0 = contextlib.ExitStack()

    def sb(es, name, shape, dt):
        return es.enter_context(nc.sbuf_tensor("s_" + name, list(shape), dt))

    ps = [es0.enter_context(nc.psum_tensor("ps%d" % i, [128, 512], F32)) for i in range(8)]

    def psb(i):
        return ps[i][:].bitcast(BF16)

    cF = sb(es0, "cF", [128, K_W], F32)
    cB = sb(es0, "cB", [128, K_W], BF16)
    mixT = sb(es0, "mixT", [128, 8, NT], BF16)
    vld = sb(es0, "vld", [128, 1], F32)
    stat = sb(es0, "stat", [128, 64], F32)
    kxT = sb(es0, "kxT", [128, 8, 256], BF16)
    vx = sb(es0, "vx", [128, 2, D], BF16)
    P.dma("sp", cF[:], consts_d, writes=["cF"])
    P.dma("pool", cB[:], consts_d, writes=["cB"])
    P.dma("sp", vld[:], vld_d, writes=["vld"])
    identB = cB[:, K_ID:K_ID + 128]
    onesB = cB[:, K_ONES:K_ONES + 128]
    maskB = cB[:, K_MASK:K_MASK + 128]
    identF = cF[:, K_ID:K_ID + 128]
    triF = cF[:, K_TRIF:K_TRIF + 128]
    tri2 = cF[:, K_TRI2:K_TRI2 + 128]
    onesF = cF[:, K_ONES:K_ONES + 128]
    ind2 = cF[:, K_IND2:K_IND2 + 2]
    tri2x4 = cF[:, K_TRI2X4:K_TRI2X4 + 512]

    def dump(name, ap, shape, dt, reads=()):
        if name in debug:
            P.barrier()
            t = nc.dram_tensor("dbg_" + name, list(shape), dt, kind="ExternalOutput").ap()
            dbg_d[name] = t
            P.dma("sp", t, ap, "d_dbg", reads=list(reads))

    def norm_stats(src_ap, src_keys, junk, jkey, scol):
        k0, k1, k2 = "st%d" % scol, "st%d" % (scol + 1), "st%d" % (scol + 2)
        P.op("act", lambda e: e.activation(junk[:], src_ap, AF.Square, accum_out=stat[:, scol:scol + 1]),
             reads=src_keys, writes=[jkey, k0])
        P.op("act", lambda e: e.activation(stat[:, scol + 1:scol + 2], stat[:, scol:scol + 1], AF.Ln, bias=EPS, scale=1.0 / D),
             reads=[k0], writes=[k1])
        P.op("act", lambda e: e.activation(stat[:, scol + 2:scol + 3], stat[:, scol + 1:scol + 2], AF.Exp, scale=-0.5),
             reads=[k1], writes=[k2])

    def norm_tail(src_ap, src_keys, gain_ap, gain_key, hb, hbkey, dstT, dst_keys, psbank, scol, copy_eng):
        k2 = "st%d" % (scol + 2)
        P.op("dve", lambda e: e.scalar_tensor_tensor(hb[:], src_ap, stat[:, scol + 2:scol + 3], gain_ap, ALU.mult, ALU.mult),
             reads=src_keys + [k2, gain_key], writes=[hbkey])
        pT = psb(psbank)
        P.group("pe", [(lambda e, c=c: e.transpose(pT[:, c * 128:(c + 1) * 128], hb[:, c * 128:(c + 1) * 128], identB)) for c in range(8)],
                reads=[hbkey, "cB"], writes=["ps%d" % psbank])
        src = pT.rearrange("p (c t) -> p c t", c=8)
        if copy_eng == "act":
            P.op("act", lambda e: e.activation(dstT, src, AF.Copy), reads=["ps%d" % psbank], writes=dst_keys)
        else:
            P.op("dve", lambda e: e.tensor_copy(dstT, src), reads=["ps%d" % psbank], writes=dst_keys)

    def norm_to_T(src_ap, src_keys, gain_ap, gain_key, hb, hbkey, junk, jkey, dstT, dst_keys, psbank, scol, copy_eng):
        norm_stats(src_ap, src_keys, junk, jkey, scol)
        norm_tail(src_ap, src_keys, gain_ap, gain_key, hb, hbkey, dstT, dst_keys, psbank, scol, copy_eng)

    es1 = contextlib.ExitStack()
    hnT = sb(es1, "hnT", [128, 8, NB * 128], BF16)
    esAB = contextlib.ExitStack()
    whg = sb(esAB, "whg", [128, 8, 2048], BF16)
    for j in range(4):
        P.dma("pool", whg[:, :, j * 512:(j + 1) * 512], w_in_v[:, :, C_GQ + j * 512:C_GQ + (j + 1) * 512], "d_whg", writes=["whg"])

    with contextlib.ExitStack() as es:
        xt = [sb(es, "xt%d" % i, [128, D], F32) for i in range(4)]
        gmix = sb(es, "gmix", [128, D], F32)
        hbA = [sb(es, "hbA%d" % i, [128, D], BF16) for i in range(2)]
        junkA = sb(es, "junkA", [128, D], BF16)
        P.dma("sp", gmix[:], gmix_d, writes=["gmix"])

        wkv = [sb(es, "wkv%d" % i, [128, 8, 512], BF16) for i in range(2)]
        memT = sb(es, "memT", [128, 8, 256], BF16)
        mt = [sb(es, "mt%d" % i, [128, D], F32) for i in range(2)]
        gmem = sb(es, "gmem", [128, D], F32)
        hbM = [sb(es, "hbM%d" % i, [128, D], BF16) for i in range(2)]
        junkM = sb(es, "junkM", [128, D], BF16)
        P.dma("sp", gmem[:], gmem_d, writes=["gmem"])

        def mem_block(mb):
            P.dma("sp", mt[mb][:], memd[mb * 128:(mb + 1) * 128, :], writes=["mt%d" % mb])
            norm_to_T(mt[mb][:], ["mt%d" % mb], gmem[:], "gmem", hbM[mb], "hbM%d" % mb, junkM, "junkM",
                      memT[:, :, mb * 128:(mb + 1) * 128], ["memT%d" % mb], 4 + mb, 24 + mb * 3, "dve")

        def kv_part(part):
            wb = wkv[part % 2]
            wk = "wkv%d" % (part % 2)
            P.dma("pool", wb[:], w_xkv_v[:, :, part * 512:(part + 1) * 512], "d_" + wk, writes=[wk])
            if part < 2:
                def kx(jj):
                    j = part * 4 + jj
                    bank = jj % 2
                    P.group("pe", [(lambda e, c=c: e.matmul(ps[bank][:, 0:256], wb[:, c, jj * 128:(jj + 1) * 128], memT[:, c, :], start=(c == 0), stop=(c == 7)))
                                   for c in range(8)], reads=["memT0", "memT1", wk], writes=["ps%d" % bank])
                    P.op("dve", lambda e: e.tensor_copy(kxT[:, j, :], ps[bank][:, 0:256]), reads=["ps%d" % bank], writes=["kxT"])
                for jj in range(4):
                    kx(jj)
            else:
                n = part - 2

                def vxb(mb):
                    bank = 2 + mb
                    P.group("pe", [(lambda e, c=c: e.matmul(ps[bank][:], memT[:, c, mb * 128:(mb + 1) * 128], wb[:, c, :], start=(c == 0), stop=(c == 7)))
                                   for c in range(8)], reads=["memT%d" % mb, wk], writes=["ps%d" % bank])
                    P.op("act", lambda e: e.activation(vx[:, mb, n * 512:(n + 1) * 512], ps[bank][:], AF.Copy), reads=["ps%d" % bank], writes=["vx"])
                vxb(0)
                vxb(1)
        def phaseA_load(b):
            src = xp[b * 128:(b + 1) * 128, :] if b < 16 else xo[(b - 16) * 128:(b - 15) * 128, :]
            s = b % 4
            P.dma("sp", xt[s][:], src, "d_xt%d" % s, writes=["xt%d" % s])
            norm_stats(xt[s][:], ["xt%d" % s], junkA, "junkA", (b % 8) * 3)

        def phaseA(b):
            s = b % 4
            norm_tail(xt[s][:], ["xt%d" % s], gmix[:], "gmix", hbA[b % 2], "hbA%d" % (b % 2),
                      hnT[:, :, b * 128:(b + 1) * 128], ["hnT%d" % b], b % 2, (b % 8) * 3,
                      "act" if b % 2 else "dve")
        phaseA_load(0)
        phaseA_load(1)
        for b in range(NB):
            if b + 2 < NB:
                phaseA_load(b + 2)
            phaseA(b)
            if b == 5:
                mem_block(0)
            if b == 7:
                mem_block(1)
            if b in (10, 14, 18, 22):
                kv_part((b - 10) // 4)
        dump("hnT", hnT[:], [128, 8, NB * 128], BF16, ["hnT%d" % b for b in range(NB)])
        P.barrier()
        P.flush()

    with contextlib.ExitStack() as es:
        NS = 3
        lb = sb(es, "lb", [128, 512], F32)
        oml = sb(es, "oml", [128, 512], F32)
        tmpl = sb(es, "tmpl", [128, 512], F32)
        gncol = sb(es, "gncol", [128, 1], F32)
        S32 = sb(es, "S32", [128, 4, 128], F32)
        S16 = [sb(es, "S16_%d" % i, [128, 4, 128], BF16) for i in range(2)]
        stmp = sb(es, "stmp", [128, 4, 128], F32)
        Ft = [sb(es, "Ft%d" % i, [128, 512], F32) for i in range(NS)]
        Gt = [sb(es, "Gt%d" % i, [128, 512], F32) for i in range(NS)]
        Em = [sb(es, "Em%d" % i, [128, 512], F32) for i in range(NS)]
        R2 = [sb(es, "R2%d" % i, [128, 512], F32) for i in range(NS)]
        R3 = [sb(es, "R3%d" % i, [128, 512], F32) for i in range(NS)]
        EB = [sb(es, "EB%d" % i, [128, 8], F32) for i in range(NS)]
        KT = [sb(es, "KT%d" % i, [128, 512], BF16) for i in range(NS)]
        QT = [sb(es, "QT%d" % i, [128, 512], BF16) for i in range(NS)]
        GI = [sb(es, "GI%d" % i, [128, 512], BF16) for i in range(NS)]
        REC = [sb(es, "REC%d" % i, [128, 512], BF16) for i in range(NS)]
        KTT = [sb(es, "KTT%d" % i, [128, 512], BF16) for i in range(NS)]
        QTT = [sb(es, "QTT%d" % i, [128, 512], BF16) for i in range(NS)]
        AT = [sb(es, "AT%d" % i, [128, 512], BF16) for i in range(NS)]
        junkB = sb(es, "junkB", [128, 128], F32)
        rs4 = [sb(es, "rs4%d" % i, [128, 12], F32) for i in range(NS)]
        P.dma("sp", lb[:], lb0_d, writes=["lb"])
        P.dma("sp", tmpl[:], lb1_d, writes=["tmpl"])
        P.dma("sp", gncol[:], gncol_d, writes=["gncol"])
        P.op("dve", lambda e: e.tensor_tensor(tmpl[:], tmpl[:], lb[:], ALU.subtract), reads=["tmpl", "lb"], writes=["tmpl"])
        P.op("act", lambda e: e.activation(tmpl[:], tmpl[:], AF.Exp), reads=["tmpl"], writes=["tmpl"])
        P.op("dve", lambda e: e.tensor_scalar(tmpl[:], tmpl[:], 1.0, None, ALU.add), reads=["tmpl"], writes=["tmpl"])
        P.op("dve", lambda e: e.reciprocal(lb[:], tmpl[:]), reads=["tmpl"], writes=["lb"])
        P.op("dve", lambda e: e.tensor_scalar(oml[:], lb[:], -1.0, 1.0, ALU.mult, ALU.add), reads=["lb"], writes=["oml"])
        P.op("pool", lambda e: e.memset(S32[:], 0.0), writes=["S32"])
        P.op("pool", lambda e: e.memset(S16[0][:], 0.0), writes=["S16_0"])
        BG, BI, BQ, BC, BA, BS = 0, 1, 2, 3, 4, 5

        def proj_tok(bank, b, col0):
            P.group("pe", [(lambda e, c=c: e.matmul(ps[bank][:], hnT[:, c, b * 128:(b + 1) * 128], whg[:, c, col0:col0 + 512],
                                                      start=(c == 0), stop=(c == 7))) for c in range(8)],
                    reads=["hnT%d" % b, "whg"], writes=["ps%d" % bank])

        def front(b):
            own = b >= 16
            s = b % NS
            ft, gt, em, r2, r3, eb = Ft[s], Gt[s], Em[s], R2[s], R3[s], EB[s]
            kt, qt, gi, ktt, qtt, at = KT[s], QT[s], GI[s], KTT[s], QTT[s], AT[s]
            kf, kg, ke, k2, k3, keb = "Ft%d" % s, "Gt%d" % s, "Em%d" % s, "R2%d" % s, "R3%d" % s, "EB%d" % s
            kkt, kqt, kgi, kktt, kqtt, kat = "KT%d" % s, "QT%d" % s, "GI%d" % s, "KTT%d" % s, "QTT%d" % s, "AT%d" % s
            proj_tok(BG, b, 512); yield
            proj_tok(BI, b, 1024); yield
            P.op("act", lambda e: e.activation(ft[:], ps[BG][:], AF.Exp, scale=-1.0), reads=["ps%d" % BG], writes=[kf]); yield
            P.op("act", lambda e: e.activation(gi[:], ps[BI][:], AF.Copy), reads=["ps%d" % BI], writes=[kgi]); yield
            if own:
                proj_tok(BQ, b, 0); yield
            P.op("dve", lambda e: e.tensor_scalar(ft[:], ft[:], 1.0, None, ALU.add), reads=[kf], writes=[kf]); yield
            P.op("dve", lambda e: e.reciprocal(ft[:], ft[:]), reads=[kf], writes=[kf]); yield
            P.op("dve", lambda e: e.scalar_tensor_tensor(ft[:], ft[:], 1.0, oml[:], ALU.subtract, ALU.mult), reads=[kf, "oml"], writes=[kf]); yield
            P.op("act", lambda e: e.activation(gt[:], ft[:], AF.Ln, bias=1.0), reads=[kf], writes=[kg]); yield
            if own:
                P.op("act", lambda e: e.activation(r2[:], ps[BQ][:], AF.Exp, scale=-1.0), reads=["ps%d" % BQ], writes=[k2]); yield
            P.op("pe", lambda e: e.matmul(ps[BC][:], tri2, gt[:], start=True, stop=True), reads=[kg, "cF"], writes=["ps%d" % BC]); yield
            P.group("pe", [(lambda e, h=h: e.matmul(ps[BA][:, h * 2:h * 2 + 2], gt[:, h * 128:(h + 1) * 128], ind2, start=True, stop=True))
                           for h in range(4)], reads=[kg, "cF"], writes=["ps%d" % BA]); yield
            if own:
                proj_tok(BG, b, 1536); yield
            P.op("act", lambda e: e.activation(em[:], ps[BC][:], AF.Exp, scale=-1.0), reads=["ps%d" % BC], writes=[ke]); yield
            P.op("act", lambda e: e.activation(eb[:], ps[BA][:, 0:8], AF.Exp), reads=["ps%d" % BA], writes=[keb]); yield
            P.op("dve", lambda e: e.scalar_tensor_tensor(kt[:], ft[:], -1.0, em[:], ALU.mult, ALU.mult), reads=[kf, ke], writes=[kkt]); yield
            if own:
                P.op("act", lambda e: e.activation(r2[:], r2[:], AF.Ln, bias=1.0), reads=[k2], writes=[k2]); yield
                P.op("dve", lambda e: e.tensor_tensor(r2[:], ps[BC][:], r2[:], ALU.subtract), reads=[k2, "ps%d" % BC], writes=[k2]); yield
                P.op("act", lambda e: e.activation(r2[:], r2[:], AF.Exp), reads=[k2], writes=[k2]); yield
                P.op("dve", lambda e: e.tensor_tensor(qt[:], ps[BQ][:], r2[:], ALU.mult), reads=[k2, "ps%d" % BQ], writes=[kqt]); yield
                P.op("act", lambda e: e.activation(r3[:], ps[BG][:], AF.Exp, scale=-1.0), reads=["ps%d" % BG], writes=[k3]); yield
                P.op("dve", lambda e: e.tensor_scalar(r3[:], r3[:], 1.0, None, ALU.add), reads=[k3], writes=[k3]); yield
                P.op("dve", lambda e: e.reciprocal(r3[:], r3[:]), reads=[k3], writes=[k3]); yield
                P.op("dve", lambda e: e.tensor_tensor(r3[:], ps[BG][:], r3[:], ALU.mult), reads=[k3, "ps%d" % BG], writes=[k3]); yield
                pq = psb(BI)
                P.group("pe", [(lambda e, h=h: e.transpose(pq[:, h * 128:(h + 1) * 128], qt[:, h * 128:(h + 1) * 128], identB)) for h in range(4)] +
                              [(lambda e, h=h: e.transpose(pq[:, 512 + h * 128:512 + (h + 1) * 128], kt[:, h * 128:(h + 1) * 128], identB)) for h in range(4)],
                        reads=[kqt, kkt, "cB"], writes=["ps%d" % BI]); yield
                P.op("act", lambda e: e.activation(qtt[:], pq[:, 0:512], AF.Copy), reads=["ps%d" % BI], writes=[kqtt]); yield
                P.op("dve", lambda e: e.tensor_copy(ktt[:], pq[:, 512:1024]), reads=["ps%d" % BI], writes=[kktt]); yield
                P.group("pe", [(lambda e, h=h: e.matmul(ps[BA][:, h * 128:(h + 1) * 128], ktt[:, h * 128:(h + 1) * 128], qtt[:, h * 128:(h + 1) * 128],
                                                         start=True, stop=True)) for h in range(4)],
                        reads=[kktt, kqtt], writes=["ps%d" % BA]); yield
                P.op("dve", lambda e: e.tensor_tensor(at[:], ps[BA][:], tri2x4, ALU.mult), reads=["ps%d" % BA, "cF"], writes=[kat]); yield

        sidx = [0]

        def state(b):
            own = b >= 16
            s = b % NS
            kt, gi, eb, qtt, at = KT[s], GI[s], EB[s], QTT[s], AT[s]
            kkt, kgi, keb, kqtt, kat = "KT%d" % s, "GI%d" % s, "EB%d" % s, "QTT%d" % s, "AT%d" % s
            bo = 6 + b % 2
            s0 = sidx[0]
            for c in range(2):
                cur = sidx[0]
                nxt = 1 - cur
                P.group("pe", [(lambda e, h=h, c=c: e.matmul(ps[BS][:, h * 128:(h + 1) * 128], kt[c * 64:(c + 1) * 64, h * 128:(h + 1) * 128],
                                                              gi[c * 64:(c + 1) * 64, h * 128:(h + 1) * 128], start=True, stop=True)) for h in range(4)],
                        reads=[kkt, kgi], writes=["ps%d" % BS]); yield
                P.op("dve", lambda e: e.tensor_tensor(stmp[:], ps[BS][:].rearrange("p (h v) -> p h v", h=4), S32[:], ALU.add),
                     reads=["ps%d" % BS, "S32"], writes=["stmp"]); yield
                ebb = eb[:, c:8:2].unsqueeze(2).to_broadcast([128, 4, 128])
                P.op("dve", lambda e, ebb=ebb: e.tensor_tensor(S32[:], stmp[:], ebb, ALU.mult), reads=["stmp", keb], writes=["S32"]); yield
                P.op("act", lambda e, nxt=nxt: e.activation(S16[nxt][:], S32[:], AF.Copy), reads=["S32"], writes=["S16_%d" % nxt]); yield
                sidx[0] = nxt
                if own and c == 0:
                    fns = []
                    for h in range(4):
                        fns.append(lambda e, h=h: e.matmul(ps[bo][:, h * 128:(h + 1) * 128], at[:, h * 128:(h + 1) * 128], gi[:, h * 128:(h + 1) * 128],
                                                            start=True, stop=False))
                        fns.append(lambda e, h=h: e.matmul(ps[bo][0:64, h * 128:(h + 1) * 128], qtt[:, h * 128:h * 128 + 64], S16[s0][:, h, :],
                                                            start=False, stop=True))
                        fns.append(lambda e, h=h, nxt=nxt: e.matmul(ps[bo][64:128, h * 128:(h + 1) * 128], qtt[:, h * 128 + 64:(h + 1) * 128], S16[nxt][:, h, :],
                                                                     start=False, stop=True, tile_position=(0, 64)))
                    P.group("pe", fns, reads=[kat, kgi, kqtt, "S16_%d" % s0, "S16_%d" % nxt], writes=["ps%d" % bo]); yield

        def output(b):
            s = b % NS
            ob = b - 16
            bo = 6 + b % 2
            r3, rec, rs = R3[s], REC[s], rs4[s]
            k3, krec, krs = "R3%d" % s, "REC%d" % s, "rs4%d" % s
            for h in range(4):
                P.op("act", lambda e, h=h: e.activation(junkB[:], ps[bo][:, h * 128:(h + 1) * 128], AF.Square, accum_out=rs[:, h:h + 1]),
                     reads=["ps%d" % bo], writes=["junkB", krs + "a%d" % h]); yield
            P.op("act", lambda e: e.activation(rs[:, 4:8], rs[:, 0:4], AF.Ln, bias=EPS, scale=1.0 / 128), reads=[krs + "a%d" % h for h in range(4)], writes=[krs + "b"]); yield
            P.op("act", lambda e: e.activation(rs[:, 8:12], rs[:, 4:8], AF.Exp, scale=-0.5), reads=[krs + "b"], writes=[krs + "c"]); yield
            for h in range(4):
                P.op("dve", lambda e, h=h: e.scalar_tensor_tensor(rec[:, h * 128:(h + 1) * 128], ps[bo][:, h * 128:(h + 1) * 128], rs[:, 8 + h:9 + h],
                                                                  r3[:, h * 128:(h + 1) * 128], ALU.mult, ALU.mult),
                     reads=["ps%d" % bo, krs + "c", k3], writes=[krec + "_%d" % h]); yield
            pr = psb(bo)
            P.group("pe", [(lambda e, h=h: e.transpose(pr[:, h * 128:(h + 1) * 128], rec[:, h * 128:(h + 1) * 128], identB)) for h in range(4)],
                    reads=[krec + "_%d" % h for h in range(4)] + ["cB"], writes=["ps%d" % bo]); yield
            P.op("act", lambda e: e.activation(mixT[:, 4:8, ob * 128:(ob + 1) * 128], pr[:, 0:512].rearrange("p (h t) -> p h t", h=4), AF.Copy, scale=gncol[:, 0:1]),
                 reads=["ps%d" % bo, "gncol"], writes=["mixH%d" % ob]); yield

        for step in range(NB + 2):
            streams = []
            if step < NB:
                streams.append(front(step))
            if 0 <= step - 1 < NB:
                streams.append(state(step - 1))
            if 16 <= step - 2 < NB:
                streams.append(output(step - 2))
            while streams:
                for g_ in list(streams):
                    try:
                        next(g_)
                    except StopIteration:
                        streams.remove(g_)
        dump("mixH", mixT[:, 4:8, :], [128, 4, NT], BF16)
        P.barrier()
        P.flush()

    esAB.close()

    with contextlib.ExitStack() as es:
        Vp = sb(es, "Vp", [128, NB, 8, 65], BF16)
        NEGF = sb(es, "NEGF", [128, 256], F32)
        FTb = sb(es, "FTb", [8, NT], BF16)
        KTh = [sb(es, "KTh%d" % i, [65, NB * 128], BF16) for i in range(2)]
        QTh = [sb(es, "QTh%d" % i, [65, NT], BF16) for i in range(2)]
        onesrow = sb(es, "onesrow", [65, 64], F32)
        esC0 = contextlib.ExitStack()
        wv = sb(esC0, "wv", [128, 8, 512], BF16)
        wff = sb(esC0, "wff", [128, 8, 8], BF16)
        fb32 = sb(esC0, "fb32", [128, 256], F32)
        LF = sb(esC0, "LF", [128, 256], F32)
        CAR = sb(esC0, "CAR", [128, 256], F32)

        P.dma("pool", wv[:], w_in_v[:, :, C_FV:C_FV + 512], writes=["wv"])
        P.dma("pool", wff[:], w_in_v[:, :, C_FF:C_FF + 8], writes=["wff"])
        P.dma("sp", fb32[:], fb32_d, writes=["fb32"])
        P.op("pool", lambda e: e.memset(Vp[:, :, :, 64:65], 1.0), writes=["Vones"])
        P.op("pool", lambda e: e.tensor_scalar(Vp[:, 0:16, :, 64:65], Vp[:, 0:16, :, 64:65], vld[:, 0:1], None, ALU.mult),
             reads=["Vones", "vld"], writes=["Vones"])
        P.op("pool", lambda e: e.memset(onesrow[:], 1.0), writes=["onesrow"])
        P.op("pool", lambda e: e.memset(KTh[0][64:65, :], 1.0), writes=["KTones0"])
        P.op("pool", lambda e: e.memset(KTh[1][64:65, :], 1.0), writes=["KTones1"])

        def v_block(b):
            bank = b % 2
            P.group("pe", [(lambda e, c=c: e.matmul(ps[bank][:], hnT[:, c, b * 128:(b + 1) * 128], wv[:, c, :], start=(c == 0), stop=(c == 7)))
                           for c in range(8)], reads=["hnT%d" % b, "wv"], writes=["ps%d" % bank])
            dst = Vp[:, b, :, 0:64]
            src = ps[bank][:].rearrange("p (h d) -> p h d", h=8)
            if b % 2 == 0:
                P.op("act", lambda e: e.activation(dst, src, AF.Copy), reads=["ps%d" % bank], writes=["V%d" % b])
            else:
                P.op("dve", lambda e: e.tensor_copy(dst, src), reads=["ps%d" % bank], writes=["V%d" % b])
            P.group("pe", [(lambda e, c=c: e.matmul(ps[2][:, b * 8:(b + 1) * 8], hnT[:, c, b * 128:(b + 1) * 128], wff[:, c, :], start=(c == 0), stop=(c == 7)))
                           for c in range(8)], reads=["hnT%d" % b, "wff"], writes=["ps2"])
        for b in range(NB):
            v_block(b)
        P.op("dve", lambda e: e.tensor_tensor(LF[:], ps[2][:, 0:256], fb32[:], ALU.add), reads=["ps2", "fb32"], writes=["LF"])
        P.op("act", lambda e: e.activation(LF[:], LF[:], AF.Exp, scale=-1.0), reads=["LF"], writes=["LF"])
        P.op("act", lambda e: e.activation(LF[:], LF[:], AF.Ln, bias=1.0), reads=["LF"], writes=["LF"])
        P.op("dve", lambda e: e.tensor_scalar(LF[:], LF[:], -1.0, None, ALU.mult), reads=["LF"], writes=["LF"])
        P.op("pe", lambda e: e.matmul(ps[3][:, 0:256], triF, LF[:], start=True, stop=True), reads=["LF", "cF"], writes=["ps3"])
        P.op("pe", lambda e: e.matmul(ps[4][:, 0:256], onesF, LF[:], start=True, stop=True), reads=["LF", "cF"], writes=["ps4"])
        P.op("dve", lambda e: e.memset(CAR[:, 0:8], 0.0), writes=["CAR"])

        def carry(b):
            P.op("dve", lambda e: e.tensor_tensor(CAR[:, b * 8:(b + 1) * 8], CAR[:, (b - 1) * 8:b * 8], ps[4][:, (b - 1) * 8:b * 8], ALU.add),
                 reads=["CAR", "ps4"], writes=["CAR"])
        for b in range(1, NB):
            carry(b)
        P.op("dve", lambda e: e.scalar_tensor_tensor(NEGF[:], ps[3][:, 0:256], -1.0, CAR[:], ALU.mult, ALU.subtract),
             reads=["ps3", "CAR"], writes=["NEGF"])
        P.op("dve", lambda e: e.tensor_scalar(LF[:], NEGF[:], -1.0, None, ALU.mult), reads=["NEGF", "LF"], writes=["LF"])

        def ft_block(ob):
            b = 16 + ob
            bank = 5 + (ob // 4) % 2
            P.op("pe", lambda e: e.matmul(ps[bank][0:8, (ob % 4) * 128:(ob % 4 + 1) * 128], LF[:, b * 8:(b + 1) * 8], identF, start=True, stop=True),
                 reads=["LF", "cF"], writes=["ps%d" % bank])
            if ob % 4 == 3:
                q = ob // 4
                P.op("dve", lambda e: e.tensor_copy(FTb[:, q * 512:(q + 1) * 512], ps[bank][0:8, :]), reads=["ps%d" % bank], writes=["FTb"])
        for ob in range(NOB):
            ft_block(ob)
        dump("negF", NEGF[:], [128, 256], F32, ["NEGF"])
        P.barrier()
        P.flush()
        esC0.close()
        wqk = sb(es, "wqk", [128, 8, 256], BF16)
        PT = [sb(es, "PT%d" % i, [128, 512], BF16) for i in range(5)]
        rrow = sb(es, "rrow", [65, 512], F32)
        tmpO = sb(es, "tmpO", [64, 512], F32)
        stB = [sb(es, "stB%d" % i, [64, 512], BF16) for i in range(2)]

        scale = 0.125

        def kproj(i, t):
            bank = t % 2
            P.group("pe", [(lambda e, c=c: e.matmul(ps[bank][0:64, :], wqk[:, c, 128 + i * 64:128 + (i + 1) * 64], hnT[:, c, t * 512:(t + 1) * 512],
                                                     start=(c == 0), stop=(c == 7))) for c in range(8)],
                    reads=["hnT%d" % (4 * t + j) for j in range(4)] + ["wqk"], writes=["ps%d" % bank])
            P.op("dve", lambda e: e.tensor_copy(KTh[i][0:64, t * 512:(t + 1) * 512], ps[bank][0:64, :]),
                 reads=["ps%d" % bank], writes=["KT%d_%d" % (i, t)])

        def qproj(i, t):
            bank = t % 2
            P.group("pe", [(lambda e, c=c: e.matmul(ps[bank][0:64, :], wqk[:, c, i * 64:(i + 1) * 64], hnT[:, c, 2048 + t * 512:2048 + (t + 1) * 512],
                                                     start=(c == 0), stop=(c == 7))) for c in range(8)],
                    reads=["hnT%d" % (16 + 4 * t + j) for j in range(4)] + ["wqk"], writes=["ps%d" % bank])
            P.op("dve", lambda e: e.tensor_scalar(QTh[i][0:64, t * 512:(t + 1) * 512], ps[bank][0:64, :], scale, None, ALU.mult),
                 reads=["ps%d" % bank], writes=["QT%d_%d" % (i, t)])

        def pair(p):
            P.dma("pool", wqk[:, :, 0:128], w_in_v[:, :, C_FQ + p * 128:C_FQ + (p + 1) * 128], "d_wqk", writes=["wqk"])
            P.dma("pool", wqk[:, :, 128:256], w_in_v[:, :, C_FK + p * 128:C_FK + (p + 1) * 128], "d_wqk", writes=["wqk"])
            for i in range(2):
                h = 2 * p + i
                P.dma("sp", QTh[i][64:65, :], FTb[h:h + 1, :], "d_ft%d" % i, reads=["FTb"], writes=["QTa%d" % i])
                for t in range(8):
                    kproj(i, t)
                for t in range(4):
                    qproj(i, t)
            items = []
            for i in range(2):
                for qi in range(4):
                    nk = 16 + 4 * qi + 4
                    for kb in range(nk):
                        items.append((i, qi, kb, nk))

            def emit_qk(n):
                i, qi, kb, nk = items[n]
                bank = n % 5
                j = kb - (16 + 4 * qi)
                c0 = max(j, 0) * 128
                fns = [lambda e: e.matmul(ps[bank][:, c0:512], KTh[i][0:65, kb * 128:(kb + 1) * 128], QTh[i][0:65, qi * 512 + c0:(qi + 1) * 512],
                                          start=True, stop=(j < 0))]
                if j >= 0:
                    fns.append(lambda e: e.matmul(ps[bank][:, c0:c0 + 128], identB, maskB, start=False, stop=True))
                P.group("pe", fns, reads=["KT%d_%d" % (i, kb // 4), "KTones%d" % i, "QT%d_%d" % (i, qi), "QTa%d" % i, "cB"], writes=["ps%d" % bank])

            def emit_rest(n):
                i, qi, kb, nk = items[n]
                h = 2 * p + i
                bank = n % 5
                pt = PT[n % 5]
                ptk = "PT%d" % (n % 5)
                j = kb - (16 + 4 * qi)
                c0 = max(j, 0) * 128
                P.op("act", lambda e: e.activation(pt[:, c0:512], ps[bank][:, c0:512], AF.Exp, bias=NEGF[:, kb * 8 + h:kb * 8 + h + 1]),
                     reads=["ps%d" % bank, "NEGF"], writes=[ptk])
                ob_ = 5 + (i * 4 + qi) % 2
                P.op("pe", lambda e: e.matmul(ps[ob_][0:65, c0:512], Vp[:, kb, h, :], pt[:, c0:512], start=(kb == 0), stop=(kb == nk - 1)),
                     reads=[ptk, "V%d" % kb, "Vones"], writes=["ps%d" % ob_])
                if kb == nk - 1:
                    P.op("dve", lambda e: e.reciprocal(rrow[64:65, :], ps[ob_][64:65, :]), reads=["ps%d" % ob_], writes=["rrow"])
                    P.op("pe", lambda e: e.matmul(ps[7][0:64, :], onesrow[64:65, :], rrow[64:65, :], start=True, stop=True), reads=["rrow", "onesrow"], writes=["ps7"])
                    P.op("dve", lambda e: e.tensor_copy(tmpO[:], ps[ob_][0:64, :]), reads=["ps%d" % ob_], writes=["tmpO"])
                    if i == 0:
                        P.op("dve", lambda e: e.tensor_tensor(mixT[0:64, p, qi * 512:(qi + 1) * 512], tmpO[:], ps[7][0:64, :], ALU.mult),
                             reads=["tmpO", "ps7"], writes=["mixFa%d_%d" % (p, qi)])
                    else:
                        sbf = stB[qi % 2]
                        P.op("dve", lambda e: e.tensor_tensor(sbf[:], tmpO[:], ps[7][0:64, :], ALU.mult),
                             reads=["tmpO", "ps7"], writes=["stB%d" % (qi % 2)])
                        P.dma("sp", mixT[64:128, p, qi * 512:(qi + 1) * 512], sbf[:], "d_stb%d" % (qi % 2), reads=["stB%d" % (qi % 2)], writes=["mixFb%d_%d" % (p, qi)])

            LA = 3
            for n in range(LA):
                emit_qk(n)
            for n in range(len(items)):
                if n + LA < len(items):
                    emit_qk(n + LA)
                emit_rest(n)
        for p in range(4):
            pair(p)
        dump("mixF", mixT[:, 0:4, :], [128, 4, NT], BF16)
        P.barrier()
        P.flush()
    es1.close()

    es2 = contextlib.ExitStack()
    y = sb(es2, "y", [128, NOB, D], F32)
    h2T = sb(es2, "h2T", [128, 8, NT], BF16)
    esD = contextlib.ExitStack()
    wq = sb(esD, "wq", [128, 8, D], BF16)
    wxo = sb(esD, "wxo", [128, 8, D], BF16)

    def add_proj(b, lhs_fn, lhs_keys, W, wkey, nchunks):
        def half(n):
            bank = (2 * b + n) % 4
            P.group("pe", [(lambda e, c=c: e.matmul(ps[bank][:], lhs_fn(c), W[:, c, n * 512:(n + 1) * 512], start=(c == 0), stop=(c == nchunks - 1)))
                           for c in range(nchunks)], reads=lhs_keys + [wkey], writes=["ps%d" % bank])
            P.op("dve", lambda e: e.tensor_tensor(y[:, b, n * 512:(n + 1) * 512], ps[bank][:], y[:, b, n * 512:(n + 1) * 512], ALU.add),
                 reads=["ps%d" % bank, "y%d" % b], writes=["y%d" % b])
        half(0)
        half(1)

    with contextlib.ExitStack() as es:
        wo = sb(es, "wo", [128, 8, D], BF16)
        gx = sb(es, "gx", [128, D], F32)
        hbD = [sb(es, "hbD%d" % i, [128, D], BF16) for i in range(2)]
        junkD = sb(es, "junkD", [128, D], BF16)
        for j in range(2):
            P.dma("pool", wo[:, :, j * 512:(j + 1) * 512], w_out_v[:, :, j * 512:(j + 1) * 512], "d_wo", writes=["wo"])
        P.dma("sp", gx[:], gx_d, writes=["gx"])
        for j in range(2):
            P.dma("pool", wq[:, :, j * 512:(j + 1) * 512], w_xq_v[:, :, j * 512:(j + 1) * 512], "d_wq", writes=["wq"])
        for j in range(2):
            P.dma("pool", wxo[:, :, j * 512:(j + 1) * 512], w_xo_v[:, :, j * 512:(j + 1) * 512], "d_wxo", writes=["wxo"])
        for b in range(NOB):
            P.dma("sp", y[:, b, :], xo[b * 128:(b + 1) * 128, :], "d_y%d" % b, writes=["y%d" % b])

        def d0_proj(b):
            keys = ["mixH%d" % b] + ["mixFa%d_%d" % (p, b // 4) for p in range(4)] + ["mixFb%d_%d" % (p, b // 4) for p in range(4)]
            add_proj(b, (lambda c: mixT[:, c, b * 128:(b + 1) * 128]), keys, wo, "wo", 8)
            norm_stats(y[:, b, :], ["y%d" % b], junkD, "junkD", (b % 8) * 3)

        def d0_norm(b):
            norm_tail(y[:, b, :], ["y%d" % b], gx[:], "gx", hbD[b % 2], "hbD%d" % (b % 2),
                      h2T[:, :, b * 128:(b + 1) * 128], ["h2T%d" % b], 4 + b % 2, (b % 8) * 3,
                      "act" if b % 2 else "dve")
        d0_proj(0)
        for b in range(NOB):
            if b + 1 < NOB:
                d0_proj(b + 1)
            d0_norm(b)
        dump("y1", y[:], [128, NOB, D], F32, ["y%d" % b for b in range(NOB)])
        P.barrier()
        P.flush()

    with contextlib.ExitStack() as es:
        gff = sb(es, "gff", [128, D], F32)
        P.dma("sp", gff[:], gff_d, writes=["gff"])
        qxT = sb(es, "qxT", [128, 8, 512], BF16)
        oxT = sb(es, "oxT", [128, 8, 512], BF16)
        PTx = [sb(es, "PTx%d" % i, [128, 2, 512], BF16) for i in range(2)]
        rden = sb(es, "rden", [128, 512], F32)
        hbE = [sb(es, "hbE%d" % i, [128, D], BF16) for i in range(2)]
        junkE = sb(es, "junkE", [128, D], BF16)

        def d1_tile(T):
            tk = ["h2T%d" % (4 * T + j) for j in range(4)]

            def qx(j):
                bank = j % 2
                P.group("pe", [(lambda e, c=c: e.matmul(ps[bank][:], wq[:, c, j * 128:(j + 1) * 128], h2T[:, c, T * 512:(T + 1) * 512], start=(c == 0), stop=(c == 7)))
                               for c in range(8)], reads=tk + ["wq"], writes=["ps%d" % bank])
                if j % 2 == 0:
                    P.op("act", lambda e: e.activation(qxT[:, j, :], ps[bank][:], AF.Copy, scale=1.0 / 16), reads=["ps%d" % bank], writes=["qxT%d" % j])
                else:
                    P.op("dve", lambda e: e.tensor_scalar(qxT[:, j, :], ps[bank][:], 1.0 / 16, None, ALU.mult), reads=["ps%d" % bank], writes=["qxT%d" % j])
            for j in range(8):
                qx(j)

            def head(hx, part):
                ptx = PTx[hx % 2]
                pk = "PTx%d" % (hx % 2)

                def sc(mb):
                    bank = 2 + mb
                    P.group("pe", [(lambda e, dc=dc: e.matmul(ps[bank][:], kxT[:, hx * 2 + dc, mb * 128:(mb + 1) * 128], qxT[:, hx * 2 + dc, :], start=(dc == 0), stop=(dc == 1)))
                                   for dc in range(2)], reads=["kxT", "qxT%d" % (hx * 2), "qxT%d" % (hx * 2 + 1)], writes=["ps%d" % bank])
                    P.op("act", lambda e: e.activation(ptx[:, mb, :], ps[bank][:], AF.Exp), reads=["ps%d" % bank], writes=[pk + "_%d" % mb])
                if part == 0:
                    sc(0)
                    sc(1)
                    return
                P.group("pe", [(lambda e, mb=mb: e.matmul(ps[4][:], onesB, ptx[:, mb, :], start=(mb == 0), stop=(mb == 1))) for mb in range(2)],
                        reads=[pk + "_0", pk + "_1", "cB"], writes=["ps4"])
                P.op("dve", lambda e: e.reciprocal(rden[:], ps[4][:]), reads=["ps4"], writes=["rden"])

                def ov(dc):
                    bank = 5 + dc
                    P.group("pe", [(lambda e, mb=mb: e.matmul(ps[bank][:], vx[:, mb, (hx * 2 + dc) * 128:(hx * 2 + dc + 1) * 128], ptx[:, mb, :], start=(mb == 0), stop=(mb == 1)))
                                   for mb in range(2)], reads=[pk + "_0", pk + "_1", "vx"], writes=["ps%d" % bank])
                    P.op("dve", lambda e: e.tensor_tensor(oxT[:, hx * 2 + dc, :], ps[bank][:], rden[:], ALU.mult),
                         reads=["ps%d" % bank, "rden"], writes=["oxT%d" % (hx * 2 + dc)])
                ov(0)
                ov(1)
            head(0, 0)
            for hx in range(4):
                if hx + 1 < 4:
                    head(hx + 1, 0)
                head(hx, 1)

            def blk_proj(bl):
                b = 4 * T + bl
                add_proj(b, (lambda c: oxT[:, c, bl * 128:(bl + 1) * 128]), ["oxT%d" % c for c in range(8)], wxo, "wxo", 8)
                norm_stats(y[:, b, :], ["y%d" % b], junkE, "junkE", (b % 8) * 3)

            def blk_norm(bl):
                b = 4 * T + bl
                norm_tail(y[:, b, :], ["y%d" % b], gff[:], "gff", hbE[b % 2], "hbE%d" % (b % 2),
                          h2T[:, :, b * 128:(b + 1) * 128], ["h2T%d" % b], 6 + b % 2, (b % 8) * 3,
                          "act" if b % 2 else "dve")
            blk_proj(0)
            for bl in range(4):
                if bl + 1 < 4:
                    blk_proj(bl + 1)
                blk_norm(bl)
        for T in range(4):
            d1_tile(T)
        dump("y2", y[:], [128, NOB, D], F32)
        P.barrier()
        P.flush()

    esD.close()

    with contextlib.ExitStack() as es:
        w1b = [sb(es, "w1b%d" % i, [128, 8, 512], BF16) for i in range(2)]
        w2b = [sb(es, "w2b%d" % i, [128, 4, D], BF16) for i in range(2)]
        U = [sb(es, "U%d" % i, [128, 4, 512], BF16) for i in range(2)]
        Rr = [sb(es, "Rr%d" % i, [128, 512], F32) for i in range(2)]
        gfin = sb(es, "gfin", [128, D], F32)
        ost = [sb(es, "ost%d" % i, [128, D], F32) for i in range(2)]
        junkF = sb(es, "junkF", [128, D], BF16)
        P.dma("sp", gfin[:], gfin_d, writes=["gfin"])
        NG = 8

        def ffn(g, T, part):
            wb1, wb2 = w1b[g % 2], w2b[g % 2]
            k1, k2 = "w1b%d" % (g % 2), "w2b%d" % (g % 2)
            u = U[T % 2]
            uk = "U%d" % (T % 2)
            tk = ["h2T%d" % (4 * T + j) for j in range(4)]

            def up(m):
                bank = m % 2
                rr = Rr[m % 2]
                rk = "Rr%d" % (m % 2)
                P.group("pe", [(lambda e, c=c: e.matmul(ps[bank][:], wb1[:, c, m * 128:(m + 1) * 128], h2T[:, c, T * 512:(T + 1) * 512], start=(c == 0), stop=(c == 7)))
                               for c in range(8)], reads=tk + [k1], writes=["ps%d" % bank])
                P.op("act", lambda e: e.activation(rr[:], ps[bank][:], AF.Relu), reads=["ps%d" % bank], writes=[rk])
                P.op("act", lambda e: e.activation(u[:, m, :], rr[:], AF.Square), reads=[rk], writes=[uk + "_%d" % m])
            if part == 0:
                for m in range(4):
                    up(m)
                return

            def down(bl, n):
                b = 4 * T + bl
                bank = 2 + (2 * bl + n) % 4
                P.group("pe", [(lambda e, m=m: e.matmul(ps[bank][:], u[:, m, bl * 128:(bl + 1) * 128], wb2[:, m, n * 512:(n + 1) * 512], start=(m == 0), stop=(m == 3)))
                               for m in range(4)], reads=[uk + "_%d" % m for m in range(4)] + [k2], writes=["ps%d" % bank])
                P.op("dve", lambda e: e.tensor_tensor(y[:, b, n * 512:(n + 1) * 512], ps[bank][:], y[:, b, n * 512:(n + 1) * 512], ALU.add),
                     reads=["ps%d" % bank, "y%d" % b], writes=["y%d" % b])

            def fin(b):
                sc = (b % 8) * 3
                o = ost[b % 2]
                k0, k1_, k2_ = "st%d" % sc, "st%d" % (sc + 1), "st%d" % (sc + 2)
                P.op("act", lambda e: e.activation(junkF[:], y[:, b, :], AF.Square, accum_out=stat[:, sc:sc + 1]),
                     reads=["y%d" % b], writes=["junkF", k0])
                P.op("act", lambda e: e.activation(stat[:, sc + 1:sc + 2], stat[:, sc:sc + 1], AF.Ln, bias=EPS, scale=1.0 / D),
                     reads=[k0], writes=[k1_])
                P.op("act", lambda e: e.activation(stat[:, sc + 2:sc + 3], stat[:, sc + 1:sc + 2], AF.Exp, scale=-0.5),
                     reads=[k1_], writes=[k2_])
                P.op("act", lambda e: e.activation(o[:], y[:, b, :], AF.Copy, scale=stat[:, sc + 2:sc + 3]),
                     reads=["y%d" % b, k2_], writes=["ost%d" % (b % 2)])
                P.op("pool", lambda e: e.tensor_tensor(o[:], o[:], gfin[:], ALU.mult),
                     reads=["ost%d" % (b % 2), "gfin"], writes=["ost%d" % (b % 2)])
                P.dma("sp", out_d[b * 128:(b + 1) * 128, :], o[:], "d_out%d" % (b % 2), reads=["ost%d" % (b % 2)])
            for bl in range(4):
                down(bl, 0)
                down(bl, 1)
                if g == NG - 1:
                    fin(4 * T + bl)

        def wload(g):
            P.dma("pool", w1b[g % 2][:], w1_v[:, :, g * 512:(g + 1) * 512], "d_w1_%d" % (g % 2), writes=["w1b%d" % (g % 2)])
            P.dma("pool", w2b[g % 2][:], w2_v[:, g * 4:(g + 1) * 4, :], "d_w2_%d" % (g % 2), writes=["w2b%d" % (g % 2)])
        seq = [(g, T) for g in range(NG) for T in range(4)]
        wload(0)
        ffn(0, 0, 0)
        for idx, (g, T) in enumerate(seq):
            if idx + 1 < len(seq):
                g2, T2 = seq[idx + 1]
                ffn(g2, T2, 0)
            ffn(g, T, 1)
            if T == 0 and g + 1 < NG:
                wload(g + 1)
        P.barrier()
        P.flush()
    es2.close()
    P.close()
    es0.close()
    return nc, dbg_d


def _consts():
    c = np.zeros((128, K_W), np.float32)
    s = np.arange(128)[:, None]
    t = np.arange(128)[None, :]
    c[:, K_ID:K_ID + 128] = np.eye(128)
    c[:, K_TRIF:K_TRIF + 128] = (s <= t)
    tri2 = ((s <= t) & (s // 64 == t // 64)).astype(np.float32)
    c[:, K_TRI2:K_TRI2 + 128] = tri2
    c[:, K_ONES:K_ONES + 128] = 1.0
    c[:, K_IND2:K_IND2 + 2] = (s // 64 == np.arange(2)[None, :])
    c[:, K_MASK:K_MASK + 128] = np.where(s <= t, 0.0, NEG)
    c[:, K_TRI2X4:K_TRI2X4 + 512] = np.tile(tri2, (1, 4))
    return c


def _bc(v, reps=1):
    v = np.asarray(v, np.float32).reshape(1, -1)
    return np.ascontiguousarray(np.broadcast_to(np.tile(v, (1, reps)), (128, v.shape[1] * reps)))


_CACHE = {}


def make_in_maps(x, mem, norm_mix_g, w_in, fox_f_bias, hgrn_lb_logits, hgrn_norm_g, w_out,
                 norm_x_g, norm_mem_g, w_xq, w_xkv, w_xo, norm_ff_g, w1, w2, final_norm_g):
    f = lambda a: np.ascontiguousarray(np.asarray(a, np.float32))
    x = f(x)
    mem = f(mem)
    shared = {
        "consts": _consts(),
        "gmix": _bc(norm_mix_g[0]), "gx": _bc(norm_x_g[0]), "gmem": _bc(norm_mem_g[0]),
        "gff": _bc(norm_ff_g[0]), "gfin": _bc(final_norm_g),
        "gn4": _bc(hgrn_norm_g[0], 4), "gncol": f(np.asarray(hgrn_norm_g[0]).reshape(128, 1)), "fb32": _bc(fox_f_bias[0], 32),
        "lb0": _bc(hgrn_lb_logits[0]), "lb1": _bc(hgrn_lb_logits[1]),
        "w_in": f(w_in[0]), "w_out": f(w_out[0]), "w_xq": f(w_xq[0]), "w_xkv": f(w_xkv[0]),
        "w_xo": f(w_xo[0]), "w1": f(w1[0]), "w2": f(w2[0]),
    }
    in_maps = []
    for c in range(8):
        b, half = c // 2, c % 2
        m = dict(shared)
        m["xo"] = np.ascontiguousarray(x[b, half * NT:(half + 1) * NT])
        m["xp"] = np.ascontiguousarray(x[b, 0:NP]) if half == 1 else np.zeros((NP, D), np.float32)
        m["vld"] = np.full((128, 1), float(half), np.float32)
        m["mem"] = np.ascontiguousarray(mem[b])
        in_maps.append(m)
    return in_maps


def kernel(**inputs):
    in_maps = make_in_maps(**inputs)
    if "nc" not in _CACHE:
        _CACHE["nc"] = build_program()[0]
    nc = _CACHE["nc"]
    res = run_bass_kernel_spmd(nc, in_maps, core_ids=list(range(8)))
    out = np.zeros((4, 4096, D), np.float32)
    for c in range(8):
        b, half = c // 2, c % 2
        out[b, half * NT:(half + 1) * NT] = np.asarray(res.results[c]["out"], np.float32)
    return out
```

```python
import contextlib
import os
import numpy as np
import concourse.bass as bass
import concourse.mybir as mybir
from concourse.bass_utils import run_bass_kernel_spmd

F32 = mybir.dt.float32
BF16 = mybir.dt.bfloat16
AF = mybir.ActivationFunctionType
ALU = mybir.AluOpType

D = 1024
NT = 2048
NP = 2048
NB = 32
NOB = 16
DFF = 4096
EPS = 1e-6
NEG = -30000.0
C_FQ, C_FK, C_FV, C_FF, C_GQ, C_GF, C_GI, C_GG = 0, 512, 1024, 1536, 1544, 2056, 2568, 3080
K_ID, K_TRIF, K_TRI2, K_ONES, K_IND2, K_MASK, K_TRI2X4, K_W = 0, 128, 256, 384, 512, 514, 642, 1154


class Prog:
    ENGS = ("pe", "act", "dve", "pool", "sp")

    def __init__(self, nc):
        self.nc = nc
        self.ops = {e: [] for e in self.ENGS}
        self.sems = {}
        self.cnt = {}
        self.seen = {e: {} for e in self.ENGS}
        self.lastw = {}
        self.readers = {}
        self._stack = []
        for e in self.ENGS:
            self._mksem("E_" + e)

    def _mksem(self, key):
        cm = self.nc.semaphore(key)
        h = cm.__enter__()
        self._stack.append(cm)
        self.sems[key] = h
        self.cnt[key] = 0
        return h

    def close(self):
        for cm in reversed(self._stack):
            cm.__exit__(None, None, None)

    def _deps(self, eng, reads, writes):
        deps = {}

        def add(ev):
            if ev is None:
                return
            sk, v = ev
            if deps.get(sk, 0) < v:
                deps[sk] = v
        for k in reads:
            add(self.lastw.get(k))
        for k in writes:
            add(self.lastw.get(k))
            for r in self.readers.get(k, ()):
                add(r)
        waits = []
        for sk, v in deps.items():
            if eng == "pe" and sk == "E_pe":
                continue
            if self.seen[eng].get(sk, 0) >= v:
                continue
            self.seen[eng][sk] = v
            waits.append((self.sems[sk], v))
        return waits

    def _commit(self, ev, reads, writes):
        for k in writes:
            self.lastw[k] = ev
            self.readers[k] = []
        for k in reads:
            self.readers.setdefault(k, []).append(ev)

    def op(self, eng, fn, reads=(), writes=()):
        self.group(eng, [fn], reads, writes)

    def group(self, eng, fns, reads=(), writes=()):
        reads = list(reads)
        writes = list(writes)
        waits = self._deps(eng, reads, writes)
        sk = "E_" + eng
        self.cnt[sk] += 1
        ev = (sk, self.cnt[sk])
        sem = self.sems[sk]

        def emit(e, waits=waits, fns=fns, sem=sem):
            for s, v in waits:
                e.wait_ge(s, v)
            ins = None
            for f in fns:
                ins = f(e)
            ins.then_inc(sem, 1)
        self.ops[eng].append(emit)
        self._commit(ev, reads, writes)

    def dma(self, eng, out, in_, semkey=None, reads=(), writes=()):
        reads = list(reads)
        writes = list(writes)
        waits = self._deps(eng, reads, writes)
        if semkey is None:
            self._uniq = getattr(self, "_uniq", 0) + 1
            semkey = "d_u%d" % self._uniq
        if semkey not in self.sems:
            self._mksem(semkey)
        self.cnt[semkey] += 16
        ev = (semkey, self.cnt[semkey])
        sem = self.sems[semkey]

        def emit(e, waits=waits, sem=sem, out=out, in_=in_):
            for s, v in waits:
                e.wait_ge(s, v)
            e.dma_start(out=out, in_=in_).then_inc(sem, 16)
        self.ops[eng].append(emit)
        self._commit(ev, reads, writes)

    def barrier(self):
        for eng in self.ENGS:
            waits = []
            for sk, h in self.sems.items():
                v = self.cnt[sk]
                if v > self.seen[eng].get(sk, 0):
                    self.seen[eng][sk] = v
                    waits.append((h, v))

            def emit(e, waits=waits):
                for s, v in waits:
                    e.wait_ge(s, v)
            self.ops[eng].append(emit)

    def flush(self):
        nc = self.nc
        ops = self.ops
        self.ops = {e: [] for e in self.ENGS}
        with nc.Block() as block:
            @block.tensor
            def _(e):
                for f in ops["pe"]:
                    f(e)

            @block.scalar
            def _(e):
                for f in ops["act"]:
                    f(e)

            @block.vector
            def _(e):
                for f in ops["dve"]:
                    f(e)

            @block.gpsimd
            def _(e):
                for f in ops["pool"]:
                    f(e)

            @block.sync
            def _(e):
                for f in ops["sp"]:
                    f(e)


def build_program(debug=()):
    nc = bass.Bass("TRN2", target_bir_lowering=False)

    def din(name, shape):
        return nc.dram_tensor(name, list(shape), F32, kind="ExternalInput").ap()
    xo = din("xo", [NT, D])
    xp = din("xp", [NP, D])
    memd = din("mem", [256, D])
    vld_d = din("vld", [128, 1])
    consts_d = din("consts", [128, K_W])
    gmix_d = din("gmix", [128, D])
    gx_d = din("gx", [128, D])
    gmem_d = din("gmem", [128, D])
    gff_d = din("gff", [128, D])
    gfin_d = din("gfin", [128, D])
    gn4_d = din("gn4", [128, 512])
    gncol_d = din("gncol", [128, 1])
    fb32_d = din("fb32", [128, 256])
    lb0_d = din("lb0", [128, 512])
    lb1_d = din("lb1", [128, 512])
    w_in = din("w_in", [D, 3592])
    w_out = din("w_out", [D, D])
    w_xq = din("w_xq", [D, D])
    w_xkv = din("w_xkv", [D, 2 * D])
    w_xo = din("w_xo", [D, D])
    w1 = din("w1", [D, DFF])
    w2 = din("w2", [DFF, D])
    out_d = nc.dram_tensor("out", [NT, D], F32, kind="ExternalOutput").ap()
    dbg_d = {}

    def wview(w):
        return w.rearrange("(k p) n -> p k n", p=128)
    w_in_v, w_out_v, w_xq_v, w_xkv_v, w_xo_v, w1_v, w2_v = map(wview, (w_in, w_out, w_xq, w_xkv, w_xo, w1, w2))

    P = Prog(nc)
    es0 = contextlib.ExitStack()

    def sb(es, name, shape, dt):
        return es.enter_context(nc.sbuf_tensor("s_" + name, list(shape), dt))

    ps = [es0.enter_context(nc.psum_tensor("ps%d" % i, [128, 512], F32)) for i in range(8)]

    def psb(i):
        return ps[i][:].bitcast(BF16)

    cF = sb(es0, "cF", [128, K_W], F32)
    cB = sb(es0, "cB", [128, K_W], BF16)
    mixT = sb(es0, "mixT", [128, 8, NT], BF16)
    vld = sb(es0, "vld", [128, 1], F32)
    stat = sb(es0, "stat", [128, 64], F32)
    kxT = sb(es0, "kxT", [128, 8, 256], BF16)
    vx = sb(es0, "vx", [128, 2, D], BF16)
    P.dma("sp", cF[:], consts_d, writes=["cF"])
    P.dma("pool", cB[:], consts_d, writes=["cB"])
    P.dma("sp", vld[:], vld_d, writes=["vld"])
    identB = cB[:, K_ID:K_ID + 128]
    onesB = cB[:, K_ONES:K_ONES + 128]
    maskB = cB[:, K_MASK:K_MASK + 128]
    identF = cF[:, K_ID:K_ID + 128]
    triF = cF[:, K_TRIF:K_TRIF + 128]
    tri2 = cF[:, K_TRI2:K_TRI2 + 128]
    onesF = cF[:, K_ONES:K_ONES + 128]
    ind2 = cF[:, K_IND2:K_IND2 + 2]
    tri2x4 = cF[:, K_TRI2X4:K_TRI2X4 + 512]

    def dump(name, ap, shape, dt, reads=()):
        if name in debug:
            P.barrier()
            t = nc.dram_tensor("dbg_" + name, list(shape), dt, kind="ExternalOutput").ap()
            dbg_d[name] = t
            P.dma("sp", t, ap, "d_dbg", reads=list(reads))

    def norm_stats(src_ap, src_keys, junk, jkey, scol):
        k0, k1, k2 = "st%d" % scol, "st%d" % (scol + 1), "st%d" % (scol + 2)
        P.op("act", lambda e: e.activation(junk[:], src_ap, AF.Square, accum_out=stat[:, scol:scol + 1]),
             reads=src_keys, writes=[jkey, k0])
        P.op("act", lambda e: e.activation(stat[:, scol + 1:scol + 2], stat[:, scol:scol + 1], AF.Ln, bias=EPS, scale=1.0 / D),
             reads=[k0], writes=[k1])
        P.op("act", lambda e: e.activation(stat[:, scol + 2:scol + 3], stat[:, scol + 1:scol + 2], AF.Exp, scale=-0.5),
             reads=[k1], writes=[k2])

    def norm_tail(src_ap, src_keys, gain_ap, gain_key, hb, hbkey, dstT, dst_keys, psbank, scol, copy_eng):
        k2 = "st%d" % (scol + 2)
        P.op("dve", lambda e: e.scalar_tensor_tensor(hb[:], src_ap, stat[:, scol + 2:scol + 3], gain_ap, ALU.mult, ALU.mult),
             reads=src_keys + [k2, gain_key], writes=[hbkey])
        pT = psb(psbank)
        P.group("pe", [(lambda e, c=c: e.transpose(pT[:, c * 128:(c + 1) * 128], hb[:, c * 128:(c + 1) * 128], identB)) for c in range(8)],
                reads=[hbkey, "cB"], writes=["ps%d" % psbank])
        src = pT.rearrange("p (c t) -> p c t", c=8)
        if copy_eng == "act":
            P.op("act", lambda e: e.activation(dstT, src, AF.Copy), reads=["ps%d" % psbank], writes=dst_keys)
        else:
            P.op("dve", lambda e: e.tensor_copy(dstT, src), reads=["ps%d" % psbank], writes=dst_keys)

    def norm_to_T(src_ap, src_keys, gain_ap, gain_key, hb, hbkey, junk, jkey, dstT, dst_keys, psbank, scol, copy_eng):
        norm_stats(src_ap, src_keys, junk, jkey, scol)
        norm_tail(src_ap, src_keys, gain_ap, gain_key, hb, hbkey, dstT, dst_keys, psbank, scol, copy_eng)

    es1 = contextlib.ExitStack()
    hnT = sb(es1, "hnT", [128, 8, NB * 128], BF16)
    esAB = contextlib.ExitStack()
    whg = sb(esAB, "whg", [128, 8, 2048], BF16)
    for j in range(4):
        P.dma("pool", whg[:, :, j * 512:(j + 1) * 512], w_in_v[:, :, C_GQ + j * 512:C_GQ + (j + 1) * 512], "d_whg", writes=["whg"])

    with contextlib.ExitStack() as es:
        xt = [sb(es, "xt%d" % i, [128, D], F32) for i in range(4)]
        gmix = sb(es, "gmix", [128, D], F32)
        hbA = [sb(es, "hbA%d" % i, [128, D], BF16) for i in range(2)]
        junkA = sb(es, "junkA", [128, D], BF16)
        P.dma("sp", gmix[:], gmix_d, writes=["gmix"])

        wkv = [sb(es, "wkv%d" % i, [128, 8, 512], BF16) for i in range(2)]
        memT = sb(es, "memT", [128, 8, 256], BF16)
        mt = [sb(es, "mt%d" % i, [128, D], F32) for i in range(2)]
        gmem = sb(es, "gmem", [128, D], F32)
        hbM = [sb(es, "hbM%d" % i, [128, D], BF16) for i in range(2)]
        junkM = sb(es, "junkM", [128, D], BF16)
        P.dma("sp", gmem[:], gmem_d, writes=["gmem"])

        def mem_block(mb):
            P.dma("sp", mt[mb][:], memd[mb * 128:(mb + 1) * 128, :], writes=["mt%d" % mb])
            norm_to_T(mt[mb][:], ["mt%d" % mb], gmem[:], "gmem", hbM[mb], "hbM%d" % mb, junkM, "junkM",
                      memT[:, :, mb * 128:(mb + 1) * 128], ["memT%d" % mb], 4 + mb, 24 + mb * 3, "dve")

        def kv_part(part):
            wb = wkv[part % 2]
            wk = "wkv%d" % (part % 2)
            P.dma("pool", wb[:], w_xkv_v[:, :, part * 512:(part + 1) * 512], "d_" + wk, writes=[wk])
            if part < 2:
                def kx(jj):
                    j = part * 4 + jj
                    bank = jj % 2
                    P.group("pe", [(lambda e, c=c: e.matmul(ps[bank][:, 0:256], wb[:, c, jj * 128:(jj + 1) * 128], memT[:, c, :], start=(c == 0), stop=(c == 7)))
                                   for c in range(8)], reads=["memT0", "memT1", wk], writes=["ps%d" % bank])
                    P.op("dve", lambda e: e.tensor_copy(kxT[:, j, :], ps[bank][:, 0:256]), reads=["ps%d" % bank], writes=["kxT"])
                for jj in range(4):
                    kx(jj)
            else:
                n = part - 2

                def vxb(mb):
                    bank = 2 + mb
                    P.group("pe", [(lambda e, c=c: e.matmul(ps[bank][:], memT[:, c, mb * 128:(mb + 1) * 128], wb[:, c, :], start=(c == 0), stop=(c == 7)))
                                   for c in range(8)], reads=["memT%d" % mb, wk], writes=["ps%d" % bank])
                    P.op("act", lambda e: e.activation(vx[:, mb, n * 512:(n + 1) * 512], ps[bank][:], AF.Copy), reads=["ps%d" % bank], writes=["vx"])
                vxb(0)
                vxb(1)
        def phaseA_load(b):
            src = xp[b * 128:(b + 1) * 128, :] if b < 16 else xo[(b - 16) * 128:(b - 15) * 128, :]
            s = b % 4
            P.dma("sp", xt[s][:], src, "d_xt%d" % s, writes=["xt%d" % s])
            norm_stats(xt[s][:], ["xt%d" % s], junkA, "junkA", (b % 8) * 3)

        def phaseA(b):
            s = b % 4
            norm_tail(xt[s][:], ["xt%d" % s], gmix[:], "gmix", hbA[b % 2], "hbA%d" % (b % 2),
                      hnT[:, :, b * 128:(b + 1) * 128], ["hnT%d" % b], b % 2, (b % 8) * 3,
                      "act" if b % 2 else "dve")
        phaseA_load(0)
        phaseA_load(1)
        for b in range(NB):
            if b + 2 < NB:
                phaseA_load(b + 2)
            phaseA(b)
            if b == 5:
                mem_block(0)
            if b == 7:
                mem_block(1)
            if b in (10, 14, 18, 22):
                kv_part((b - 10) // 4)
        dump("hnT", hnT[:], [128, 8, NB * 128], BF16, ["hnT%d" % b for b in range(NB)])
        P.barrier()
        P.flush()

    with contextlib.ExitStack() as es:
        NS = 3
        lb = sb(es, "lb", [128, 512], F32)
        oml = sb(es, "oml", [128, 512], F32)
        tmpl = sb(es, "tmpl", [128, 512], F32)
        gncol = sb(es, "gncol", [128, 1], F32)
        S32 = sb(es, "S32", [128, 4, 128], F32)
        S16 = [sb(es, "S16_%d" % i, [128, 4, 128], BF16) for i in range(2)]
        stmp = sb(es, "stmp", [128, 4, 128], F32)
        Ft = [sb(es, "Ft%d" % i, [128, 512], F32) for i in range(NS)]
        Gt = [sb(es, "Gt%d" % i, [128, 512], F32) for i in range(NS)]
        Em = [sb(es, "Em%d" % i, [128, 512], F32) for i in range(NS)]
        R2 = [sb(es, "R2%d" % i, [128, 512], F32) for i in range(NS)]
        R3 = [sb(es, "R3%d" % i, [128, 512], F32) for i in range(NS)]
        EB = [sb(es, "EB%d" % i, [128, 8], F32) for i in range(NS)]
        KT = [sb(es, "KT%d" % i, [128, 512], BF16) for i in range(NS)]
        QT = [sb(es, "QT%d" % i, [128, 512], BF16) for i in range(NS)]
        GI = [sb(es, "GI%d" % i, [128, 512], BF16) for i in range(NS)]
        REC = [sb(es, "REC%d" % i, [128, 512], BF16) for i in range(NS)]
        KTT = [sb(es, "KTT%d" % i, [128, 512], BF16) for i in range(NS)]
        QTT = [sb(es, "QTT%d" % i, [128, 512], BF16) for i in range(NS)]
        AT = [sb(es, "AT%d" % i, [128, 512], BF16) for i in range(NS)]
        junkB = sb(es, "junkB", [128, 128], F32)
        rs4 = [sb(es, "rs4%d" % i, [128, 12], F32) for i in range(NS)]
        P.dma("sp", lb[:], lb0_d, writes=["lb"])
        P.dma("sp", tmpl[:], lb1_d, writes=["tmpl"])
        P.dma("sp", gncol[:], gncol_d, writes=["gncol"])
        P.op("dve", lambda e: e.tensor_tensor(tmpl[:], tmpl[:], lb[:], ALU.subtract), reads=["tmpl", "lb"], writes=["tmpl"])
        P.op("act", lambda e: e.activation(tmpl[:], tmpl[:], AF.Exp), reads=["tmpl"], writes=["tmpl"])
        P.op("dve", lambda e: e.tensor_scalar(tmpl[:], tmpl[:], 1.0, None, ALU.add), reads=["tmpl"], writes=["tmpl"])
        P.op("dve", lambda e: e.reciprocal(lb[:], tmpl[:]), reads=["tmpl"], writes=["lb"])
        P.op("dve", lambda e: e.tensor_scalar(oml[:], lb[:], -1.0, 1.0, ALU.mult, ALU.add), reads=["lb"], writes=["oml"])
        P.op("pool", lambda e: e.memset(S32[:], 0.0), writes=["S32"])
        P.op("pool", lambda e: e.memset(S16[0][:], 0.0), writes=["S16_0"])
        BG, BI, BQ, BC, BA, BS = 0, 1, 2, 3, 4, 5

        def proj_tok(bank, b, col0):
            P.group("pe", [(lambda e, c=c: e.matmul(ps[bank][:], hnT[:, c, b * 128:(b + 1) * 128], whg[:, c, col0:col0 + 512],
                                                      start=(c == 0), stop=(c == 7))) for c in range(8)],
                    reads=["hnT%d" % b, "whg"], writes=["ps%d" % bank])

        def front(b):
            own = b >= 16
            s = b % NS
            ft, gt, em, r2, r3, eb = Ft[s], Gt[s], Em[s], R2[s], R3[s], EB[s]
            kt, qt, gi, ktt, qtt, at = KT[s], QT[s], GI[s], KTT[s], QTT[s], AT[s]
            kf, kg, ke, k2, k3, keb = "Ft%d" % s, "Gt%d" % s, "Em%d" % s, "R2%d" % s, "R3%d" % s, "EB%d" % s
            kkt, kqt, kgi, kktt, kqtt, kat = "KT%d" % s, "QT%d" % s, "GI%d" % s, "KTT%d" % s, "QTT%d" % s, "AT%d" % s
            proj_tok(BG, b, 512); yield
            proj_tok(BI, b, 1024); yield
            P.op("act", lambda e: e.activation(ft[:], ps[BG][:], AF.Exp, scale=-1.0), reads=["ps%d" % BG], writes=[kf]); yield
            P.op("act", lambda e: e.activation(gi[:], ps[BI][:], AF.Copy), reads=["ps%d" % BI], writes=[kgi]); yield
            if own:
                proj_tok(BQ, b, 0); yield
            P.op("dve", lambda e: e.tensor_scalar(ft[:], ft[:], 1.0, None, ALU.add), reads=[kf], writes=[kf]); yield
            P.op("dve", lambda e: e.reciprocal(ft[:], ft[:]), reads=[kf], writes=[kf]); yield
            P.op("dve", lambda e: e.scalar_tensor_tensor(ft[:], ft[:], 1.0, oml[:], ALU.subtract, ALU.mult), reads=[kf, "oml"], writes=[kf]); yield
            P.op("act", lambda e: e.activation(gt[:], ft[:], AF.Ln, bias=1.0), reads=[kf], writes=[kg]); yield
            if own:
                P.op("act", lambda e: e.activation(r2[:], ps[BQ][:], AF.Exp, scale=-1.0), reads=["ps%d" % BQ], writes=[k2]); yield
            P.op("pe", lambda e: e.matmul(ps[BC][:], tri2, gt[:], start=True, stop=True), reads=[kg, "cF"], writes=["ps%d" % BC]); yield
            P.group("pe", [(lambda e, h=h: e.matmul(ps[BA][:, h * 2:h * 2 + 2], gt[:, h * 128:(h + 1) * 128], ind2, start=True, stop=True))
                           for h in range(4)], reads=[kg, "cF"], writes=["ps%d" % BA]); yield
            if own:
                proj_tok(BG, b, 1536); yield
            P.op("act", lambda e: e.activation(em[:], ps[BC][:], AF.Exp, scale=-1.0), reads=["ps%d" % BC], writes=[ke]); yield
            P.op("act", lambda e: e.activation(eb[:], ps[BA][:, 0:8], AF.Exp), reads=["ps%d" % BA], writes=[keb]); yield
            P.op("dve", lambda e: e.scalar_tensor_tensor(kt[:], ft[:], -1.0, em[:], ALU.mult, ALU.mult), reads=[kf, ke], writes=[kkt]); yield
            if own:
                P.op("act", lambda e: e.activation(r2[:], r2[:], AF.Ln, bias=1.0), reads=[k2], writes=[k2]); yield
                P.op("dve", lambda e: e.tensor_tensor(r2[:], ps[BC][:], r2[:], ALU.subtract), reads=[k2, "ps%d" % BC], writes=[k2]); yield
                P.op("act", lambda e: e.activation(r2[:], r2[:], AF.Exp), reads=[k2], writes=[k2]); yield
                P.op("dve", lambda e: e.tensor_tensor(qt[:], ps[BQ][:], r2[:], ALU.mult), reads=[k2, "ps%d" % BQ], writes=[kqt]); yield
                P.op("act", lambda e: e.activation(r3[:], ps[BG][:], AF.Exp, scale=-1.0), reads=["ps%d" % BG], writes=[k3]); yield
                P.op("dve", lambda e: e.tensor_scalar(r3[:], r3[:], 1.0, None, ALU.add), reads=[k3], writes=[k3]); yield
                P.op("dve", lambda e: e.reciprocal(r3[:], r3[:]), reads=[k3], writes=[k3]); yield
                P.op("dve", lambda e: e.tensor_tensor(r3[:], ps[BG][:], r3[:], ALU.mult), reads=[k3, "ps%d" % BG], writes=[k3]); yield
                pq = psb(BI)
                P.group("pe", [(lambda e, h=h: e.transpose(pq[:, h * 128:(h + 1) * 128], qt[:, h * 128:(h + 1) * 128], identB)) for h in range(4)] +
                              [(lambda e, h=h: e.transpose(pq[:, 512 + h * 128:512 + (h + 1) * 128], kt[:, h * 128:(h + 1) * 128], identB)) for h in range(4)],
                        reads=[kqt, kkt, "cB"], writes=["ps%d" % BI]); yield
                P.op("act", lambda e: e.activation(qtt[:], pq[:, 0:512], AF.Copy), reads=["ps%d" % BI], writes=[kqtt]); yield
                P.op("dve", lambda e: e.tensor_copy(ktt[:], pq[:, 512:1024]), reads=["ps%d" % BI], writes=[kktt]); yield
                P.group("pe", [(lambda e, h=h: e.matmul(ps[BA][:, h * 128:(h + 1) * 128], ktt[:, h * 128:(h + 1) * 128], qtt[:, h * 128:(h + 1) * 128],
                                                         start=True, stop=True)) for h in range(4)],
                        reads=[kktt, kqtt], writes=["ps%d" % BA]); yield
                P.op("dve", lambda e: e.tensor_tensor(at[:], ps[BA][:], tri2x4, ALU.mult), reads=["ps%d" % BA, "cF"], writes=[kat]); yield

        sidx = [0]

        def state(b):
            own = b >= 16
            s = b % NS
            kt, gi, eb, qtt, at = KT[s], GI[s], EB[s], QTT[s], AT[s]
            kkt, kgi, keb, kqtt, kat = "KT%d" % s, "GI%d" % s, "EB%d" % s, "QTT%d" % s, "AT%d" % s
            bo = 6 + b % 2
            s0 = sidx[0]
            for c in range(2):
                cur = sidx[0]
                nxt = 1 - cur
                P.group("pe", [(lambda e, h=h, c=c: e.matmul(ps[BS][:, h * 128:(h + 1) * 128], kt[c * 64:(c + 1) * 64, h * 128:(h + 1) * 128],
                                                              gi[c * 64:(c + 1) * 64, h * 128:(h + 1) * 128], start=True, stop=True)) for h in range(4)],
                        reads=[kkt, kgi], writes=["ps%d" % BS]); yield
                P.op("dve", lambda e: e.tensor_tensor(stmp[:], ps[BS][:].rearrange("p (h v) -> p h v", h=4), S32[:], ALU.add),
                     reads=["ps%d" % BS, "S32"], writes=["stmp"]); yield
                ebb = eb[:, c:8:2].unsqueeze(2).to_broadcast([128, 4, 128])
                P.op("dve", lambda e, ebb=ebb: e.tensor_tensor(S32[:], stmp[:], ebb, ALU.mult), reads=["stmp", keb], writes=["S32"]); yield
                P.op("act", lambda e, nxt=nxt: e.activation(S16[nxt][:], S32[:], AF.Copy), reads=["S32"], writes=["S16_%d" % nxt]); yield
                sidx[0] = nxt
                if own and c == 0:
                    fns = []
                    for h in range(4):
                        fns.append(lambda e, h=h: e.matmul(ps[bo][:, h * 128:(h + 1) * 128], at[:, h * 128:(h + 1) * 128], gi[:, h * 128:(h + 1) * 128],
                                                            start=True, stop=False))
                        fns.append(lambda e, h=h: e.matmul(ps[bo][0:64, h * 128:(h + 1) * 128], qtt[:, h * 128:h * 128 + 64], S16[s0][:, h, :],
                                                            start=False, stop=True))
                        fns.append(lambda e, h=h, nxt=nxt: e.matmul(ps[bo][64:128, h * 128:(h + 1) * 128], qtt[:, h * 128 + 64:(h + 1) * 128], S16[nxt][:, h, :],
                                                                     start=False, stop=True, tile_position=(0, 64)))
                    P.group("pe", fns, reads=[kat, kgi, kqtt, "S16_%d" % s0, "S16_%d" % nxt], writes=["ps%d" % bo]); yield

        def output(b):
            s = b % NS
            ob = b - 16
            bo = 6 + b % 2
            r3, rec, rs = R3[s], REC[s], rs4[s]
            k3, krec, krs = "R3%d" % s, "REC%d" % s, "rs4%d" % s
            for h in range(4):
                P.op("act", lambda e, h=h: e.activation(junkB[:], ps[bo][:, h * 128:(h + 1) * 128], AF.Square, accum_out=rs[:, h:h + 1]),
                     reads=["ps%d" % bo], writes=["junkB", krs + "a%d" % h]); yield
            P.op("act", lambda e: e.activation(rs[:, 4:8], rs[:, 0:4], AF.Ln, bias=EPS, scale=1.0 / 128), reads=[krs + "a%d" % h for h in range(4)], writes=[krs + "b"]); yield
            P.op("act", lambda e: e.activation(rs[:, 8:12], rs[:, 4:8], AF.Exp, scale=-0.5), reads=[krs + "b"], writes=[krs + "c"]); yield
            for h in range(4):
                P.op("dve", lambda e, h=h: e.scalar_tensor_tensor(rec[:, h * 128:(h + 1) * 128], ps[bo][:, h * 128:(h + 1) * 128], rs[:, 8 + h:9 + h],
                                                                  r3[:, h * 128:(h + 1) * 128], ALU.mult, ALU.mult),
                     reads=["ps%d" % bo, krs + "c", k3], writes=[krec + "_%d" % h]); yield
            pr = psb(bo)
            P.group("pe", [(lambda e, h=h: e.transpose(pr[:, h * 128:(h + 1) * 128], rec[:, h * 128:(h + 1) * 128], identB)) for h in range(4)],
                    reads=[krec + "_%d" % h for h in range(4)] + ["cB"], writes=["ps%d" % bo]); yield
            P.op("act", lambda e: e.activation(mixT[:, 4:8, ob * 128:(ob + 1) * 128], pr[:, 0:512].rearrange("p (h t) -> p h t", h=4), AF.Copy, scale=gncol[:, 0:1]),
                 reads=["ps%d" % bo, "gncol"], writes=["mixH%d" % ob]); yield

        for step in range(NB + 2):
            streams = []
            if step < NB:
                streams.append(front(step))
            if 0 <= step - 1 < NB:
                streams.append(state(step - 1))
            if 16 <= step - 2 < NB:
                streams.append(output(step - 2))
            while streams:
                for g_ in list(streams):
                    try:
                        next(g_)
                    except StopIteration:
                        streams.remove(g_)
        dump("mixH", mixT[:, 4:8, :], [128, 4, NT], BF16)
        P.barrier()
        P.flush()

    esAB.close()

    with contextlib.ExitStack() as es:
        Vp = sb(es, "Vp", [128, NB, 8, 65], BF16)
        NEGF = sb(es, "NEGF", [128, 256], F32)
        FTb = sb(es, "FTb", [8, NT], BF16)
        KTh = [sb(es, "KTh%d" % i, [65, NB * 128], BF16) for i in range(2)]
        QTh = [sb(es, "QTh%d" % i, [65, NT], BF16) for i in range(2)]
        onesrow = sb(es, "onesrow", [65, 64], F32)
        esC0 = contextlib.ExitStack()
        wv = sb(esC0, "wv", [128, 8, 512], BF16)
        wff = sb(esC0, "wff", [128, 8, 8], BF16)
        fb32 = sb(esC0, "fb32", [128, 256], F32)
        LF = sb(esC0, "LF", [128, 256], F32)
        CAR = sb(esC0, "CAR", [128, 256], F32)

        P.dma("pool", wv[:], w_in_v[:, :, C_FV:C_FV + 512], writes=["wv"])
        P.dma("pool", wff[:], w_in_v[:, :, C_FF:C_FF + 8], writes=["wff"])
        P.dma("sp", fb32[:], fb32_d, writes=["fb32"])
        P.op("pool", lambda e: e.memset(Vp[:, :, :, 64:65], 1.0), writes=["Vones"])
        P.op("pool", lambda e: e.tensor_scalar(Vp[:, 0:16, :, 64:65], Vp[:, 0:16, :, 64:65], vld[:, 0:1], None, ALU.mult),
             reads=["Vones", "vld"], writes=["Vones"])
        P.op("pool", lambda e: e.memset(onesrow[:], 1.0), writes=["onesrow"])
        P.op("pool", lambda e: e.memset(KTh[0][64:65, :], 1.0), writes=["KTones0"])
        P.op("pool", lambda e: e.memset(KTh[1][64:65, :], 1.0), writes=["KTones1"])

        def v_block(b):
            bank = b % 2
            P.group("pe", [(lambda e, c=c: e.matmul(ps[bank][:], hnT[:, c, b * 128:(b + 1) * 128], wv[:, c, :], start=(c == 0), stop=(c == 7)))
                           for c in range(8)], reads=["hnT%d" % b, "wv"], writes=["ps%d" % bank])
            dst = Vp[:, b, :, 0:64]
            src = ps[bank][:].rearrange("p (h d) -> p h d", h=8)
            if b % 2 == 0:
                P.op("act", lambda e: e.activation(dst, src, AF.Copy), reads=["ps%d" % bank], writes=["V%d" % b])
            else:
                P.op("dve", lambda e: e.tensor_copy(dst, src), reads=["ps%d" % bank], writes=["V%d" % b])
            P.group("pe", [(lambda e, c=c: e.matmul(ps[2][:, b * 8:(b + 1) * 8], hnT[:, c, b * 128:(b + 1) * 128], wff[:, c, :], start=(c == 0), stop=(c == 7)))
                           for c in range(8)], reads=["hnT%d" % b, "wff"], writes=["ps2"])
        for b in range(NB):
            v_block(b)
        P.op("dve", lambda e: e.tensor_tensor(LF[:], ps[2][:, 0:256], fb32[:], ALU.add), reads=["ps2", "fb32"], writes=["LF"])
        P.op("act", lambda e: e.activation(LF[:], LF[:], AF.Exp, scale=-1.0), reads=["LF"], writes=["LF"])
        P.op("act", lambda e: e.activation(LF[:], LF[:], AF.Ln, bias=1.0), reads=["LF"], writes=["LF"])
        P.op("dve", lambda e: e.tensor_scalar(LF[:], LF[:], -1.0, None, ALU.mult), reads=["LF"], writes=["LF"])
        P.op("pe", lambda e: e.matmul(ps[3][:, 0:256], triF, LF[:], start=True, stop=True), reads=["LF", "cF"], writes=["ps3"])
        P.op("pe", lambda e: e.matmul(ps[4][:, 0:256], onesF, LF[:], start=True, stop=True), reads=["LF", "cF"], writes=["ps4"])
        P.op("dve", lambda e: e.memset(CAR[:, 0:8], 0.0), writes=["CAR"])

        def carry(b):
            P.op("dve", lambda e: e.tensor_tensor(CAR[:, b * 8:(b + 1) * 8], CAR[:, (b - 1) * 8:b * 8], ps[4][:, (b - 1) * 8:b * 8], ALU.add),
                 reads=["CAR", "ps4"], writes=["CAR"])
        for b in range(1, NB):
            carry(b)
        P.op("dve", lambda e: e.scalar_tensor_tensor(NEGF[:], ps[3][:, 0:256], -1.0, CAR[:], ALU.mult, ALU.subtract),
             reads=["ps3", "CAR"], writes=["NEGF"])
        P.op("dve", lambda e: e.tensor_scalar(LF[:], NEGF[:], -1.0, None, ALU.mult), reads=["NEGF", "LF"], writes=["LF"])

        def ft_block(ob):
            b = 16 + ob
            bank = 5 + (ob // 4) % 2
            P.op("pe", lambda e: e.matmul(ps[bank][0:8, (ob % 4) * 128:(ob % 4 + 1) * 128], LF[:, b * 8:(b + 1) * 8], identF, start=True, stop=True),
                 reads=["LF", "cF"], writes=["ps%d" % bank])
            if ob % 4 == 3:
                q = ob // 4
                P.op("dve", lambda e: e.tensor_copy(FTb[:, q * 512:(q + 1) * 512], ps[bank][0:8, :]), reads=["ps%d" % bank], writes=["FTb"])
        for ob in range(NOB):
            ft_block(ob)
        dump("negF", NEGF[:], [128, 256], F32, ["NEGF"])
        P.barrier()
        P.flush()
        esC0.close()
        wqk = sb(es, "wqk", [128, 8, 256], BF16)
        PT = [sb(es, "PT%d" % i, [128, 512], BF16) for i in range(5)]
        rrow = sb(es, "rrow", [65, 512], F32)
        tmpO = sb(es, "tmpO", [64, 512], F32)
        stB = [sb(es, "stB%d" % i, [64, 512], BF16) for i in range(2)]

        scale = 0.125

        def kproj(i, t):
            bank = t % 2
            P.group("pe", [(lambda e, c=c: e.matmul(ps[bank][0:64, :], wqk[:, c, 128 + i * 64:128 + (i + 1) * 64], hnT[:, c, t * 512:(t + 1) * 512],
                                                     start=(c == 0), stop=(c == 7))) for c in range(8)],
                    reads=["hnT%d" % (4 * t + j) for j in range(4)] + ["wqk"], writes=["ps%d" % bank])
            P.op("dve", lambda e: e.tensor_copy(KTh[i][0:64, t * 512:(t + 1) * 512], ps[bank][0:64, :]),
                 reads=["ps%d" % bank], writes=["KT%d_%d" % (i, t)])

        def qproj(i, t):
            bank = t % 2
            P.group("pe", [(lambda e, c=c: e.matmul(ps[bank][0:64, :], wqk[:, c, i * 64:(i + 1) * 64], hnT[:, c, 2048 + t * 512:2048 + (t + 1) * 512],
                                                     start=(c == 0), stop=(c == 7))) for c in range(8)],
                    reads=["hnT%d" % (16 + 4 * t + j) for j in range(4)] + ["wqk"], writes=["ps%d" % bank])
            P.op("dve", lambda e: e.tensor_scalar(QTh[i][0:64, t * 512:(t + 1) * 512], ps[bank][0:64, :], scale, None, ALU.mult),
                 reads=["ps%d" % bank], writes=["QT%d_%d" % (i, t)])

        def pair(p):
            P.dma("pool", wqk[:, :, 0:128], w_in_v[:, :, C_FQ + p * 128:C_FQ + (p + 1) * 128], "d_wqk", writes=["wqk"])
            P.dma("pool", wqk[:, :, 128:256], w_in_v[:, :, C_FK + p * 128:C_FK + (p + 1) * 128], "d_wqk", writes=["wqk"])
            for i in range(2):
                h = 2 * p + i
                P.dma("sp", QTh[i][64:65, :], FTb[h:h + 1, :], "d_ft%d" % i, reads=["FTb"], writes=["QTa%d" % i])
                for t in range(8):
                    kproj(i, t)
                for t in range(4):
                    qproj(i, t)
            items = []
            for i in range(2):
                for qi in range(4):
                    nk = 16 + 4 * qi + 4
                    for kb in range(nk):
                        items.append((i, qi, kb, nk))

            def emit_qk(n):
                i, qi, kb, nk = items[n]
                bank = n % 5
                j = kb - (16 + 4 * qi)
                c0 = max(j, 0) * 128
                fns = [lambda e: e.matmul(ps[bank][:, c0:512], KTh[i][0:65, kb * 128:(kb + 1) * 128], QTh[i][0:65, qi * 512 + c0:(qi + 1) * 512],
                                          start=True, stop=(j < 0))]
                if j >= 0:
                    fns.append(lambda e: e.matmul(ps[bank][:, c0:c0 + 128], identB, maskB, start=False, stop=True))
                P.group("pe", fns, reads=["KT%d_%d" % (i, kb // 4), "KTones%d" % i, "QT%d_%d" % (i, qi), "QTa%d" % i, "cB"], writes=["ps%d" % bank])

            def emit_rest(n):
                i, qi, kb, nk = items[n]
                h = 2 * p + i
                bank = n % 5
                pt = PT[n % 5]
                ptk = "PT%d" % (n % 5)
                j = kb - (16 + 4 * qi)
                c0 = max(j, 0) * 128
                P.op("act", lambda e: e.activation(pt[:, c0:512], ps[bank][:, c0:512], AF.Exp, bias=NEGF[:, kb * 8 + h:kb * 8 + h + 1]),
                     reads=["ps%d" % bank, "NEGF"], writes=[ptk])
                ob_ = 5 + (i * 4 + qi) % 2
                P.op("pe", lambda e: e.matmul(ps[ob_][0:65, c0:512], Vp[:, kb, h, :], pt[:, c0:512], start=(kb == 0), stop=(kb == nk - 1)),
                     reads=[ptk, "V%d" % kb, "Vones"], writes=["ps%d" % ob_])
                if kb == nk - 1:
                    P.op("dve", lambda e: e.reciprocal(rrow[64:65, :], ps[ob_][64:65, :]), reads=["ps%d" % ob_], writes=["rrow"])
                    P.op("pe", lambda e: e.matmul(ps[7][0:64, :], onesrow[64:65, :], rrow[64:65, :], start=True, stop=True), reads=["rrow", "onesrow"], writes=["ps7"])
                    P.op("dve", lambda e: e.tensor_copy(tmpO[:], ps[ob_][0:64, :]), reads=["ps%d" % ob_], writes=["tmpO"])
                    if i == 0:
                        P.op("dve", lambda e: e.tensor_tensor(mixT[0:64, p, qi * 512:(qi + 1) * 512], tmpO[:], ps[7][0:64, :], ALU.mult),
                             reads=["tmpO", "ps7"], writes=["mixFa%d_%d" % (p, qi)])
                    else:
                        sbf = stB[qi % 2]
                        P.op("dve", lambda e: e.tensor_tensor(sbf[:], tmpO[:], ps[7][0:64, :], ALU.mult),
                             reads=["tmpO", "ps7"], writes=["stB%d" % (qi % 2)])
                        P.dma("sp", mixT[64:128, p, qi * 512:(qi + 1) * 512], sbf[:], "d_stb%d" % (qi % 2), reads=["stB%d" % (qi % 2)], writes=["mixFb%d_%d" % (p, qi)])

            LA = 3
            for n in range(LA):
                emit_qk(n)
            for n in range(len(items)):
                if n + LA < len(items):
                    emit_qk(n + LA)
                emit_rest(n)
        for p in range(4):
            pair(p)
        dump("mixF", mixT[:, 0:4, :], [128, 4, NT], BF16)
        P.barrier()
        P.flush()
    es1.close()

    es2 = contextlib.ExitStack()
    y = sb(es2, "y", [128, NOB, D], F32)
    h2T = sb(es2, "h2T", [128, 8, NT], BF16)
    esD = contextlib.ExitStack()
    wq = sb(esD, "wq", [128, 8, D], BF16)
    wxo = sb(esD, "wxo", [128, 8, D], BF16)

    def add_proj(b, lhs_fn, lhs_keys, W, wkey, nchunks):
        def half(n):
            bank = (2 * b + n) % 4
            P.group("pe", [(lambda e, c=c: e.matmul(ps[bank][:], lhs_fn(c), W[:, c, n * 512:(n + 1) * 512], start=(c == 0), stop=(c == nchunks - 1)))
                           for c in range(nchunks)], reads=lhs_keys + [wkey], writes=["ps%d" % bank])
            P.op("dve", lambda e: e.tensor_tensor(y[:, b, n * 512:(n + 1) * 512], ps[bank][:], y[:, b, n * 512:(n + 1) * 512], ALU.add),
                 reads=["ps%d" % bank, "y%d" % b], writes=["y%d" % b])
        half(0)
        half(1)

    with contextlib.ExitStack() as es:
        wo = sb(es, "wo", [128, 8, D], BF16)
        gx = sb(es, "gx", [128, D], F32)
        hbD = [sb(es, "hbD%d" % i, [128, D], BF16) for i in range(2)]
        junkD = sb(es, "junkD", [128, D], BF16)
        for j in range(2):
            P.dma("pool", wo[:, :, j * 512:(j + 1) * 512], w_out_v[:, :, j * 512:(j + 1) * 512], "d_wo", writes=["wo"])
        P.dma("sp", gx[:], gx_d, writes=["gx"])
        for b in range(NOB):
            P.dma("sp", y[:, b, :], xo[b * 128:(b + 1) * 128, :], "d_y%d" % b, writes=["y%d" % b])

        def d0_proj(b):
            keys = ["mixH%d" % b] + ["mixFa%d_%d" % (p, b // 4) for p in range(4)] + ["mixFb%d_%d" % (p, b // 4) for p in range(4)]
            add_proj(b, (lambda c: mixT[:, c, b * 128:(b + 1) * 128]), keys, wo, "wo", 8)
            norm_stats(y[:, b, :], ["y%d" % b], junkD, "junkD", (b % 8) * 3)

        def d0_norm(b):
            norm_tail(y[:, b, :], ["y%d" % b], gx[:], "gx", hbD[b % 2], "hbD%d" % (b % 2),
                      h2T[:, :, b * 128:(b + 1) * 128], ["h2T%d" % b], 4 + b % 2, (b % 8) * 3,
                      "act" if b % 2 else "dve")
        d0_proj(0)
        d0_proj(1)
        for b in range(NOB):
            if b + 2 < NOB:
                d0_proj(b + 2)
            d0_norm(b)
            if b == 2:
                for j in range(2):
                    P.dma("pool", wq[:, :, j * 512:(j + 1) * 512], w_xq_v[:, :, j * 512:(j + 1) * 512], "d_wq", writes=["wq"])
            if b == 6:
                for j in range(2):
                    P.dma("pool", wxo[:, :, j * 512:(j + 1) * 512], w_xo_v[:, :, j * 512:(j + 1) * 512], "d_wxo", writes=["wxo"])
        dump("y1", y[:], [128, NOB, D], F32, ["y%d" % b for b in range(NOB)])
        P.barrier()
        P.flush()

    with contextlib.ExitStack() as es:
        gff = sb(es, "gff", [128, D], F32)
        P.dma("sp", gff[:], gff_d, writes=["gff"])
        qxT = sb(es, "qxT", [128, 8, 512], BF16)
        oxT = sb(es, "oxT", [128, 8, 512], BF16)
        PTx = [sb(es, "PTx%d" % i, [128, 2, 512], BF16) for i in range(2)]
        rden = sb(es, "rden", [128, 512], F32)
        hbE = [sb(es, "hbE%d" % i, [128, D], BF16) for i in range(2)]
        junkE = sb(es, "junkE", [128, D], BF16)

        def d1_tile(T):
            tk = ["h2T%d" % (4 * T + j) for j in range(4)]

            def qx(j):
                bank = j % 2
                P.group("pe", [(lambda e, c=c: e.matmul(ps[bank][:], wq[:, c, j * 128:(j + 1) * 128], h2T[:, c, T * 512:(T + 1) * 512], start=(c == 0), stop=(c == 7)))
                               for c in range(8)], reads=tk + ["wq"], writes=["ps%d" % bank])
                if j % 2 == 0:
                    P.op("act", lambda e: e.activation(qxT[:, j, :], ps[bank][:], AF.Copy, scale=1.0 / 16), reads=["ps%d" % bank], writes=["qxT%d" % j])
                else:
                    P.op("dve", lambda e: e.tensor_scalar(qxT[:, j, :], ps[bank][:], 1.0 / 16, None, ALU.mult), reads=["ps%d" % bank], writes=["qxT%d" % j])
            for j in range(8):
                qx(j)

            def head(hx, part):
                ptx = PTx[hx % 2]
                pk = "PTx%d" % (hx % 2)

                def sc(mb):
                    bank = 2 + mb
                    P.group("pe", [(lambda e, dc=dc: e.matmul(ps[bank][:], kxT[:, hx * 2 + dc, mb * 128:(mb + 1) * 128], qxT[:, hx * 2 + dc, :], start=(dc == 0), stop=(dc == 1)))
                                   for dc in range(2)], reads=["kxT", "qxT%d" % (hx * 2), "qxT%d" % (hx * 2 + 1)], writes=["ps%d" % bank])
                    P.op("act", lambda e: e.activation(ptx[:, mb, :], ps[bank][:], AF.Exp), reads=["ps%d" % bank], writes=[pk + "_%d" % mb])
                if part == 0:
                    sc(0)
                    sc(1)
                    return
                P.group("pe", [(lambda e, mb=mb: e.matmul(ps[4][:], onesB, ptx[:, mb, :], start=(mb == 0), stop=(mb == 1))) for mb in range(2)],
                        reads=[pk + "_0", pk + "_1", "cB"], writes=["ps4"])
                P.op("dve", lambda e: e.reciprocal(rden[:], ps[4][:]), reads=["ps4"], writes=["rden"])

                def ov(dc):
                    bank = 5 + dc
                    P.group("pe", [(lambda e, mb=mb: e.matmul(ps[bank][:], vx[:, mb, (hx * 2 + dc) * 128:(hx * 2 + dc + 1) * 128], ptx[:, mb, :], start=(mb == 0), stop=(mb == 1)))
                                   for mb in range(2)], reads=[pk + "_0", pk + "_1", "vx"], writes=["ps%d" % bank])
                    P.op("dve", lambda e: e.tensor_tensor(oxT[:, hx * 2 + dc, :], ps[bank][:], rden[:], ALU.mult),
                         reads=["ps%d" % bank, "rden"], writes=["oxT%d" % (hx * 2 + dc)])
                ov(0)
                ov(1)
            head(0, 0)
            for hx in range(4):
                if hx + 1 < 4:
                    head(hx + 1, 0)
                head(hx, 1)

            def blk_proj(bl):
                b = 4 * T + bl
                add_proj(b, (lambda c: oxT[:, c, bl * 128:(bl + 1) * 128]), ["oxT%d" % c for c in range(8)], wxo, "wxo", 8)
                norm_stats(y[:, b, :], ["y%d" % b], junkE, "junkE", (b % 8) * 3)

            def blk_norm(bl):
                b = 4 * T + bl
                norm_tail(y[:, b, :], ["y%d" % b], gff[:], "gff", hbE[b % 2], "hbE%d" % (b % 2),
                          h2T[:, :, b * 128:(b + 1) * 128], ["h2T%d" % b], 6 + b % 2, (b % 8) * 3,
                          "act" if b % 2 else "dve")
            blk_proj(0)
            blk_proj(1)
            for bl in range(4):
                if bl + 2 < 4:
                    blk_proj(bl + 2)
                blk_norm(bl)
        for T in range(4):
            d1_tile(T)
        dump("y2", y[:], [128, NOB, D], F32)
        P.barrier()
        P.flush()

    esD.close()

    with contextlib.ExitStack() as es:
        w1b = [sb(es, "w1b%d" % i, [128, 8, 512], BF16) for i in range(2)]
        w2b = [sb(es, "w2b%d" % i, [128, 4, D], BF16) for i in range(2)]
        U = [sb(es, "U%d" % i, [128, 4, 512], BF16) for i in range(2)]
        Rr = [sb(es, "Rr%d" % i, [128, 512], F32) for i in range(2)]
        gfin = sb(es, "gfin", [128, D], F32)
        ost = [sb(es, "ost%d" % i, [128, D], F32) for i in range(2)]
        junkF = sb(es, "junkF", [128, D], BF16)
        P.dma("sp", gfin[:], gfin_d, writes=["gfin"])
        NG = 8

        def ffn(g, T, part):
            wb1, wb2 = w1b[g % 2], w2b[g % 2]
            k1, k2 = "w1b%d" % (g % 2), "w2b%d" % (g % 2)
            u = U[T % 2]
            uk = "U%d" % (T % 2)
            tk = ["h2T%d" % (4 * T + j) for j in range(4)]

            def up(m):
                bank = m % 2
                rr = Rr[m % 2]
                rk = "Rr%d" % (m % 2)
                P.group("pe", [(lambda e, c=c: e.matmul(ps[bank][:], wb1[:, c, m * 128:(m + 1) * 128], h2T[:, c, T * 512:(T + 1) * 512], start=(c == 0), stop=(c == 7)))
                               for c in range(8)], reads=tk + [k1], writes=["ps%d" % bank])
                P.op("act", lambda e: e.activation(rr[:], ps[bank][:], AF.Relu), reads=["ps%d" % bank], writes=[rk])
                P.op("act", lambda e: e.activation(u[:, m, :], rr[:], AF.Square), reads=[rk], writes=[uk + "_%d" % m])
            if part == 0:
                for m in range(4):
                    up(m)
                return

            def down(bl, n):
                b = 4 * T + bl
                bank = 2 + (2 * bl + n) % 4
                P.group("pe", [(lambda e, m=m: e.matmul(ps[bank][:], u[:, m, bl * 128:(bl + 1) * 128], wb2[:, m, n * 512:(n + 1) * 512], start=(m == 0), stop=(m == 3)))
                               for m in range(4)], reads=[uk + "_%d" % m for m in range(4)] + [k2], writes=["ps%d" % bank])
                P.op("dve", lambda e: e.tensor_tensor(y[:, b, n * 512:(n + 1) * 512], ps[bank][:], y[:, b, n * 512:(n + 1) * 512], ALU.add),
                     reads=["ps%d" % bank, "y%d" % b], writes=["y%d" % b])

            def fin(b):
                sc = (b % 8) * 3
                o = ost[b % 2]
                k0, k1_, k2_ = "st%d" % sc, "st%d" % (sc + 1), "st%d" % (sc + 2)
                P.op("act", lambda e: e.activation(junkF[:], y[:, b, :], AF.Square, accum_out=stat[:, sc:sc + 1]),
                     reads=["y%d" % b], writes=["junkF", k0])
                P.op("act", lambda e: e.activation(stat[:, sc + 1:sc + 2], stat[:, sc:sc + 1], AF.Ln, bias=EPS, scale=1.0 / D),
                     reads=[k0], writes=[k1_])
                P.op("act", lambda e: e.activation(stat[:, sc + 2:sc + 3], stat[:, sc + 1:sc + 2], AF.Exp, scale=-0.5),
                     reads=[k1_], writes=[k2_])
                P.op("act", lambda e: e.activation(o[:], y[:, b, :], AF.Copy, scale=stat[:, sc + 2:sc + 3]),
                     reads=["y%d" % b, k2_], writes=["ost%d" % (b % 2)])
                P.op("pool", lambda e: e.tensor_tensor(o[:], o[:], gfin[:], ALU.mult),
                     reads=["ost%d" % (b % 2), "gfin"], writes=["ost%d" % (b % 2)])
                P.dma("sp", out_d[b * 128:(b + 1) * 128, :], o[:], "d_out%d" % (b % 2), reads=["ost%d" % (b % 2)])
            for bl in range(4):
                down(bl, 0)
                down(bl, 1)
                if g == NG - 1:
                    fin(4 * T + bl)

        def wload(g):
            P.dma("pool", w1b[g % 2][:], w1_v[:, :, g * 512:(g + 1) * 512], "d_w1_%d" % (g % 2), writes=["w1b%d" % (g % 2)])
            P.dma("pool", w2b[g % 2][:], w2_v[:, g * 4:(g + 1) * 4, :], "d_w2_%d" % (g % 2), writes=["w2b%d" % (g % 2)])
        seq = [(g, T) for g in range(NG) for T in range(4)]
        wload(0)
        ffn(0, 0, 0)
        for idx, (g, T) in enumerate(seq):
            if idx + 1 < len(seq):
                g2, T2 = seq[idx + 1]
                ffn(g2, T2, 0)
            ffn(g, T, 1)
            if T == 0 and g + 1 < NG:
                wload(g + 1)
        P.barrier()
        P.flush()
    es2.close()
    P.close()
    es0.close()
    return nc, dbg_d


def _consts():
    c = np.zeros((128, K_W), np.float32)
    s = np.arange(128)[:, None]
    t = np.arange(128)[None, :]
    c[:, K_ID:K_ID + 128] = np.eye(128)
    c[:, K_TRIF:K_TRIF + 128] = (s <= t)
    tri2 = ((s <= t) & (s // 64 == t // 64)).astype(np.float32)
    c[:, K_TRI2:K_TRI2 + 128] = tri2
    c[:, K_ONES:K_ONES + 128] = 1.0
    c[:, K_IND2:K_IND2 + 2] = (s // 64 == np.arange(2)[None, :])
    c[:, K_MASK:K_MASK + 128] = np.where(s <= t, 0.0, NEG)
    c[:, K_TRI2X4:K_TRI2X4 + 512] = np.tile(tri2, (1, 4))
    return c


def _bc(v, reps=1):
    v = np.asarray(v, np.float32).reshape(1, -1)
    return np.ascontiguousarray(np.broadcast_to(np.tile(v, (1, reps)), (128, v.shape[1] * reps)))


_CACHE = {}


def make_in_maps(x, mem, norm_mix_g, w_in, fox_f_bias, hgrn_lb_logits, hgrn_norm_g, w_out,
                 norm_x_g, norm_mem_g, w_xq, w_xkv, w_xo, norm_ff_g, w1, w2, final_norm_g):
    f = lambda a: np.ascontiguousarray(np.asarray(a, np.float32))
    x = f(x)
    mem = f(mem)
    shared = {
        "consts": _consts(),
        "gmix": _bc(norm_mix_g[0]), "gx": _bc(norm_x_g[0]), "gmem": _bc(norm_mem_g[0]),
        "gff": _bc(norm_ff_g[0]), "gfin": _bc(final_norm_g),
        "gn4": _bc(hgrn_norm_g[0], 4), "gncol": f(np.asarray(hgrn_norm_g[0]).reshape(128, 1)), "fb32": _bc(fox_f_bias[0], 32),
        "lb0": _bc(hgrn_lb_logits[0]), "lb1": _bc(hgrn_lb_logits[1]),
        "w_in": f(w_in[0]), "w_out": f(w_out[0]), "w_xq": f(w_xq[0]), "w_xkv": f(w_xkv[0]),
        "w_xo": f(w_xo[0]), "w1": f(w1[0]), "w2": f(w2[0]),
    }
    in_maps = []
    for c in range(8):
        b, half = c // 2, c % 2
        m = dict(shared)
        m["xo"] = np.ascontiguousarray(x[b, half * NT:(half + 1) * NT])
        m["xp"] = np.ascontiguousarray(x[b, 0:NP]) if half == 1 else np.zeros((NP, D), np.float32)
        m["vld"] = np.full((128, 1), float(half), np.float32)
        m["mem"] = np.ascontiguousarray(mem[b])
        in_maps.append(m)
    return in_maps


def kernel(**inputs):
    in_maps = make_in_maps(**inputs)
    if "nc" not in _CACHE:
        _CACHE["nc"] = build_program()[0]
    nc = _CACHE["nc"]
    res = run_bass_kernel_spmd(nc, in_maps, core_ids=list(range(8)))
    out = np.zeros((4, 4096, D), np.float32)
    for c in range(8):
        b, half = c // 2, c % 2
        out[b, half * NT:(half + 1) * NT] = np.asarray(res.results[c]["out"], np.float32)
    return out
```
